# Optimizing a Trainium2 kernel written in Bass

```python
import math
import jax, jax.numpy as jnp
from jax import lax
import numpy as np

D_MODEL = 2048
BATCH = 4
SEQ = 4096
DEPTH = 1

MIX_WIDTH = D_MODEL
POOL_WIDTH = MIX_WIDTH // 2
LRU_WIDTH = MIX_WIDTH - POOL_WIDTH
POOL_WINDOWS = (2, 4, 8, 16)
N_POOL_GROUPS = len(POOL_WINDOWS)
POOL_GROUP = POOL_WIDTH // N_POOL_GROUPS
LRU_HEADS = 8
LRU_HEAD_DIM = LRU_WIDTH // LRU_HEADS
CONV_WIDTH = 4
LRU_C = 8.0
N_MEM = 256
XATTN_HEADS = 4
XATTN_HEAD_DIM = D_MODEL // XATTN_HEADS
D_FF = 4 * D_MODEL
LN_EPS = 1e-5
DEEPNORM_ALPHA = (2.0 * DEPTH) ** 0.25
DEEPNORM_BETA = (8.0 * DEPTH) ** -0.25

kernel_name = "hymba_pool_rglru_deepnorm_layer"


def layer_norm(x, g, b):
    xf = x.astype(jnp.float32)
    mu = jnp.mean(xf, axis=-1, keepdims=True)
    var = jnp.mean(jnp.square(xf - mu), axis=-1, keepdims=True)
    y = (xf - mu) * lax.rsqrt(var + LN_EPS)
    return (y * g.astype(jnp.float32) + b.astype(jnp.float32)).astype(x.dtype)


def multiscale_pool(u, w_pool, b_pool, pool_scale):
    B, S, _ = u.shape
    uf = u.astype(jnp.float32).reshape(B, S, N_POOL_GROUPS, POOL_GROUP)
    csum = jnp.cumsum(uf, axis=1)
    t = jnp.arange(S)
    means = []
    for g, w in enumerate(POOL_WINDOWS):
        c = csum[:, :, g]
        c_prev = jnp.pad(c, ((0, 0), (w, 0), (0, 0)))[:, :S]
        cnt = jnp.minimum(t + 1, w).astype(jnp.float32)[None, :, None]
        means.append((c - c_prev) / cnt)
    mixed = (jnp.stack(means, axis=2) - uf).astype(u.dtype)
    y = jnp.einsum('bsgc,gcd->bsgd', mixed, w_pool) + b_pool
    return y.reshape(B, S, POOL_WIDTH) * pool_scale


def causal_depthwise_conv(x, w, b):
    S = x.shape[1]
    xp = jnp.pad(x, ((0, 0), (CONV_WIDTH - 1, 0), (0, 0)))
    y = xp[:, 0:S] * w[0]
    for k in range(1, CONV_WIDTH):
        y = y + xp[:, k:k + S] * w[k]
    return y + b


def rg_lru(x, w_a, b_a, w_x, b_x, lam):
    B, S, _ = x.shape
    xh = x.reshape(B, S, LRU_HEADS, LRU_HEAD_DIM)
    r = jax.nn.sigmoid(jnp.einsum('bshi,hij->bshj', xh, w_a) + b_a).reshape(B, S, LRU_WIDTH)
    i = jax.nn.sigmoid(jnp.einsum('bshi,hij->bshj', xh, w_x) + b_x).reshape(B, S, LRU_WIDTH)
    log_a = -LRU_C * r.astype(jnp.float32) * jax.nn.softplus(-lam.astype(jnp.float32))
    a = jnp.exp(log_a)
    mult = jnp.sqrt(-jnp.expm1(2.0 * log_a))
    mult = jnp.where((jnp.arange(S) == 0)[None, :, None], 1.0, mult)
    bterm = mult * (i * x).astype(jnp.float32)

    def combine(lhs, rhs):
        a1, b1 = lhs
        a2, b2 = rhs
        return a1 * a2, a2 * b1 + b2

    _, h = lax.associative_scan(combine, (a, bterm), axis=1)
    return h.astype(x.dtype)


def hybrid_mixer(x, w_in, conv_w, conv_b, w_a, b_a, w_x, b_x, lam,
                 w_pool, b_pool, pool_scale, w_out):
    proj = x @ w_in
    u_pool = proj[..., :POOL_WIDTH]
    u_lru = proj[..., POOL_WIDTH:POOL_WIDTH + LRU_WIDTH]
    u_gate = proj[..., POOL_WIDTH + LRU_WIDTH:]
    y_pool = multiscale_pool(u_pool, w_pool, b_pool, pool_scale)
    h = rg_lru(causal_depthwise_conv(u_lru, conv_w, conv_b), w_a, b_a, w_x, b_x, lam)
    y_lru = h * jax.nn.gelu(u_gate)
    return jnp.concatenate([y_pool, y_lru], axis=-1) @ w_out


def memory_cross_attention(x, mem, w_q, w_k, w_v, w_o):
    B, S, _ = x.shape
    M = mem.shape[1]
    q = (x @ w_q).reshape(B, S, XATTN_HEADS, XATTN_HEAD_DIM)
    k = (mem @ w_k).reshape(B, M, XATTN_HEADS, XATTN_HEAD_DIM)
    v = (mem @ w_v).reshape(B, M, XATTN_HEADS, XATTN_HEAD_DIM)
    s = jnp.einsum('bqhd,bmhd->bhqm', q, k).astype(jnp.float32) * (XATTN_HEAD_DIM ** -0.5)
    p = jax.nn.softmax(s, axis=-1).astype(v.dtype)
    o = jnp.einsum('bhqm,bmhd->bqhd', p, v).reshape(B, S, D_MODEL)
    return o @ w_o


def squared_relu_mlp(x, w1, w2):
    return jnp.square(jax.nn.relu(x @ w1)) @ w2


def setup_inputs(seed: int = 0) -> dict:
    key = jax.random.key(seed)
    ks = jax.random.split(key, 32)
    f32 = jnp.float32

    def nrm(k, shape, scale):
        return jax.random.normal(k, shape, f32) * scale

    L = DEPTH
    u = jax.random.uniform(ks[10], (L, LRU_WIDTH), f32, 0.9, 0.999)
    s = u ** (1.0 / LRU_C)
    lam = jnp.log(s) - jnp.log1p(-s)
    return {
        "x": nrm(ks[0], (BATCH, SEQ, D_MODEL), 1.0),
        "mem": nrm(ks[1], (BATCH, N_MEM, D_MODEL), 1.0),
        "w_in": nrm(ks[2], (L, D_MODEL, POOL_WIDTH + 2 * LRU_WIDTH), D_MODEL ** -0.5),
        "conv_w": nrm(ks[3], (L, CONV_WIDTH, LRU_WIDTH), CONV_WIDTH ** -0.5),
        "conv_b": nrm(ks[4], (L, LRU_WIDTH), 0.01),
        "w_a": nrm(ks[5], (L, LRU_HEADS, LRU_HEAD_DIM, LRU_HEAD_DIM), LRU_HEAD_DIM ** -0.5),
        "b_a": nrm(ks[6], (L, LRU_HEADS, LRU_HEAD_DIM), 0.01),
        "w_x": nrm(ks[7], (L, LRU_HEADS, LRU_HEAD_DIM, LRU_HEAD_DIM), LRU_HEAD_DIM ** -0.5),
        "b_x": nrm(ks[8], (L, LRU_HEADS, LRU_HEAD_DIM), 0.01),
        "lru_lambda": lam,
        "w_pool": nrm(ks[11], (L, N_POOL_GROUPS, POOL_GROUP, POOL_GROUP), POOL_GROUP ** -0.5),
        "b_pool": nrm(ks[12], (L, N_POOL_GROUPS, POOL_GROUP), 0.01),
        "pool_scale": 1.0 + nrm(ks[13], (L, POOL_WIDTH), 0.1),
        "w_out": nrm(ks[14], (L, MIX_WIDTH, D_MODEL), MIX_WIDTH ** -0.5 * DEEPNORM_BETA),
        "ln1_g": 1.0 + nrm(ks[15], (L, D_MODEL), 0.05),
        "ln1_b": nrm(ks[16], (L, D_MODEL), 0.01),
        "w_q": nrm(ks[17], (L, D_MODEL, D_MODEL), D_MODEL ** -0.5),
        "w_k": nrm(ks[18], (L, D_MODEL, D_MODEL), D_MODEL ** -0.5),
        "w_v": nrm(ks[19], (L, D_MODEL, D_MODEL), D_MODEL ** -0.5 * DEEPNORM_BETA),
        "w_o": nrm(ks[20], (L, D_MODEL, D_MODEL), D_MODEL ** -0.5 * DEEPNORM_BETA),
        "ln2_g": 1.0 + nrm(ks[21], (L, D_MODEL), 0.05),
        "ln2_b": nrm(ks[22], (L, D_MODEL), 0.01),
        "w_ff1": nrm(ks[23], (L, D_MODEL, D_FF), D_MODEL ** -0.5 * DEEPNORM_BETA),
        "w_ff2": nrm(ks[24], (L, D_FF, D_MODEL), D_FF ** -0.5 * DEEPNORM_BETA),
        "ln3_g": 1.0 + nrm(ks[25], (L, D_MODEL), 0.05),
        "ln3_b": nrm(ks[26], (L, D_MODEL), 0.01),
    }


def reference(x, mem, w_in, conv_w, conv_b, w_a, b_a, w_x, b_x, lru_lambda,
              w_pool, b_pool, pool_scale, w_out, ln1_g, ln1_b,
              w_q, w_k, w_v, w_o, ln2_g, ln2_b, w_ff1, w_ff2, ln3_g, ln3_b):
    for l in range(DEPTH):
        y = hybrid_mixer(x, w_in[l], conv_w[l], conv_b[l], w_a[l], b_a[l], w_x[l], b_x[l],
                         lru_lambda[l], w_pool[l], b_pool[l], pool_scale[l], w_out[l])
        x = layer_norm(DEEPNORM_ALPHA * x + y, ln1_g[l], ln1_b[l])
        y = memory_cross_attention(x, mem, w_q[l], w_k[l], w_v[l], w_o[l])
        x = layer_norm(DEEPNORM_ALPHA * x + y, ln2_g[l], ln2_b[l])
        y = squared_relu_mlp(x, w_ff1[l], w_ff2[l])
        x = layer_norm(DEEPNORM_ALPHA * x + y, ln3_g[l], ln3_b[l])
    return x
```

```python
import contextlib
import numpy as np
import concourse.bass as bass
import concourse.mybir as mybir
from concourse.bass_utils import run_bass_kernel_spmd

F32 = mybir.dt.float32
BF16 = mybir.dt.bfloat16
ALU = mybir.AluOpType
AF = mybir.ActivationFunctionType

D = 2048
SEQ = 4096
NB = 4
NCORE = 8
TOK = 2048
T = 1024
NPASS = 2
KC = 16
ALPHA = 2.0 ** 0.25
LN_EPS = 1e-5
NPP = 256
BW = 256
NSLOT = 4

C_CONVW, C_CONVB, C_BA, C_BX, C_LAM, C_BPOOL, C_PSCALE = 0, 32, 40, 48, 56, 64, 72
C_LN = 80
C_FLAG, C_OMF, C_RC = 176, 177, 178
P_CH, P_C, P_HBA, P_HBX, P_BPS, P_E, P_L = 0, 8, 16, 24, 32, 40, 48


class Res:
    __slots__ = ("name", "w", "r")

    def __init__(self, name):
        self.name = name
        self.w = None
        self.r = {}


class DSem:
    def __init__(self, handle):
        self.h = handle
        self.n = 0


class Prog:
    ENG = ("pe", "act", "dve", "pool", "sp")

    def __init__(self):
        self.q = {e: [] for e in self.ENG}
        self.cnt = {e: 0 for e in self.ENG}
        self.waited = {e: {} for e in self.ENG}

    def op(self, eng, fn, reads=(), writes=(), dsem=None):
        raw = {}
        oth = {}

        def add(d, ev):
            if ev is None:
                return
            k, v = ev
            if d.get(k, 0) < v:
                d[k] = v

        for r in reads:
            add(raw, r.w)
        for w in writes:
            add(oth, w.w)
            for k, v in w.r.items():
                add(oth, (k, v))
        if dsem is not None:
            dsem.n += 16
            ev = (dsem, dsem.n)
        else:
            self.cnt[eng] += 1
            ev = (eng, self.cnt[eng])
        deps = dict(raw)
        for k, v in oth.items():
            if k == eng and dsem is None:
                continue
            if deps.get(k, 0) < v:
                deps[k] = v
        waits = []
        wd = self.waited[eng]
        for k, v in deps.items():
            if wd.get(k, 0) >= v:
                continue
            wd[k] = v
            waits.append((k, v))
        self.q[eng].append((waits, fn, ev))
        for r in reads:
            if r.r.get(ev[0], 0) < ev[1]:
                r.r[ev[0]] = ev[1]
        for w in writes:
            w.w = ev
            w.r = {}
        return ev

    @staticmethod
    def inherit(srcs, dsts):
        ev = {}
        for s in srcs:
            if s.w is not None and ev.get(s.w[0], 0) < s.w[1]:
                ev[s.w[0]] = s.w[1]
            for k, v in s.r.items():
                if ev.get(k, 0) < v:
                    ev[k] = v
        for d in dsts:
            for k, v in ev.items():
                if d.r.get(k, 0) < v:
                    d.r[k] = v


def build_nc(stop=None):
    nc = bass.Bass("TRN2", target_bir_lowering=False)
    P = Prog()

    def dram(name, shape, kind="ExternalInput"):
        return nc.dram_tensor(name, list(shape), F32, kind=kind).ap()

    xT = dram("xT", [D, TOK])
    xpT = dram("xpT", [D, TOK])
    memT = dram("memT", [D, 256])
    pp_d = dram("pp", [128, NPP])
    w_in = dram("w_in", [D, 3072])
    w_out = dram("w_out", [D, D])
    w_q = dram("w_q", [D, D])
    w_k = dram("w_k", [D, D])
    w_v = dram("w_v", [D, D])
    w_o = dram("w_o", [D, D])
    w_ff1 = dram("w_ff1", [D, 4 * D])
    w_ff2 = dram("w_ff2", [4 * D, D])
    w_a = dram("w_a", [8, 128, 128])
    w_x = dram("w_x", [8, 128, 128])
    w_pool = dram("w_pool", [4, 256, 256])
    xhT = dram("xhT", [D, 16])
    outT = dram("outT", [D, TOK], kind="ExternalOutput")

    es = contextlib.ExitStack()
    with es:
        def sb(name, shape, dt):
            return es.enter_context(nc.sbuf_tensor(name, list(shape), dt))

        xf = sb("xf", [128, KC * T + 128], F32)
        xb = sb("xb", [128, KC, T], BF16)
        tb = sb("tb", [128, KC, T], BF16)
        ring = sb("ring", [128, NSLOT, KC, BW], BF16)
        kT = sb("kT", [128, KC, 256], BF16)
        vv = sb("vv", [128, 2, D], BF16)
        wa = sb("wa", [128, 8, 128], BF16)
        wx = sb("wx", [128, 8, 128], BF16)
        wp = sb("wp", [128, 4, 2, 256], BF16)
        pp = sb("ppar", [128, NPP], F32)
        dp = sb("dpar", [128, 64], F32)
        ones = sb("ones", [128, 128], BF16)
        car_h = sb("car_h", [128, 8], F32)
        car_u = sb("car_u", [128, 8, 4], F32)
        car_p = sb("car_p", [128, 8, 16], F32)
        tmp16 = sb("tmp16", [128, 16], F32)
        ksum = sb("ksum", [128, 16], F32)
        G = [sb(f"G{i}", [128, T], F32) for i in range(4)]
        Bq = [sb(f"B{i}", [128, T], BF16) for i in range(2)]
        xhb = sb("xhb", [128, KC, 16], BF16)
        PD = [es.enter_context(nc.psum_tensor(f"pd{i}", [128, T], F32)) for i in range(4)]

        sem_cnt = {e: es.enter_context(nc.semaphore(f"c_{e}")) for e in ("pe", "act", "dve", "pool")}

        def dsem(name):
            return DSem(es.enter_context(nc.semaphore(name)))

        s_ring = [dsem(f"ring{i}") for i in range(NSLOT)]
        s_xb = dsem("xbld")
        s_xfc = [dsem(f"xfld{c}") for c in range(KC)]
        s_out = dsem("outst")
        s_pp = dsem("ppld")
        s_ws = dsem("wsld")
        s_mem = dsem("memld")
        s_xh = dsem("xhld")
        s_tbx = dsem("tbxld")

        R_XF = [Res(f"xf{c}") for c in range(KC)]
        R_XB = [Res(f"xb{c}") for c in range(KC)]
        R_TB = [Res(f"tb{c}") for c in range(KC)]
        R_RING = [Res(f"ring{i}") for i in range(NSLOT)]
        R_KT, R_V, R_WS, R_PP, R_DP, R_ONES = Res("kT"), Res("v"), Res("ws"), Res("pp"), Res("dp"), Res("ones")
        R_MEMT = Res("memT")
        R_CH = [Res(f"ch{j}") for j in range(8)]
        R_CU = [Res(f"cu{j}") for j in range(8)]
        R_CP = [Res(f"cp{j}") for j in range(8)]
        R_T16 = Res("t16")
        R_KS = Res("ksum")
        R_G = [Res(f"G{i}") for i in range(4)]
        R_B = [Res(f"B{i}") for i in range(2)]
        R_XH = Res("xh")
        R_PD = [Res(f"pd{i}") for i in range(4)]
        VOFF = {"u": (0, 1040, 3), "xc": (3120, 1024, 3), "tha": (6192, 1024, 2), "a": (8240, 1024, 2),
                "thi": (10288, 1024, 2), "up": (12336, 1040, 2), "Q": (14416, 1040, 1), "R": (15456, 1040, 1)}
        R_SV = {n: [Res(f"{n}{i}") for i in range(VOFF[n][2])] for n in VOFF}
        ALL_SCR = [r for n in R_SV for r in R_SV[n]]

        def sv(name, i, lo, n):
            b = VOFF[name][0] + i * VOFF[name][1]
            return xf[:, b + lo:b + lo + n]

        def xfc(c):
            return xf[:, c * T:(c + 1) * T]

        GB = [g.bitcast(BF16) for g in G]

        def memv(k, lo, n):
            return GB[k // 8][:, (k % 8) * 256 + lo:(k % 8) * 256 + lo + n]

        state = {"d": 0, "slot": 0, "dset": [0, 1, 2, 3]}

        def nextD():
            ds = state["dset"]
            state["d"] = (state["d"] + 1) % len(ds)
            return ds[state["d"]]

        def load_block(w_ap, r0, c0):
            s = state["slot"]
            state["slot"] = (s + 1) % NSLOT
            src = w_ap[r0:r0 + D, c0:c0 + BW].rearrange("(k p) n -> p k n", p=128)
            P.op("pool", lambda e, s=s, src=src: e.dma_start(out=ring[:, s], in_=src),
                 writes=[R_RING[s]], dsem=s_ring[s])
            return s

        def mm_big(di, lhs_fn, rhs_fn, nk, reads, ntile=2, n=512):
            def fn(e):
                ins = None
                for k in range(nk):
                    l = lhs_fn(k)
                    for tt in range(ntile):
                        ins = e.matmul(PD[di][:, tt * 512:tt * 512 + n], lhsT=l, rhs=rhs_fn(k, tt),
                                       start=(k == 0), stop=(k == nk - 1))
                return ins
            P.op("pe", fn, reads=reads, writes=[R_PD[di]])

        def proj(di, slot, off, src, R_SRC):
            mm_big(di, lambda k: ring[:, slot, k, off:off + 128],
                   lambda k, tt: src[:, k, tt * 512:(tt + 1) * 512], KC,
                   reads=[R_RING[slot]] + R_SRC)

        def act(out, in_, func, reads, writes, bias=None, scale=None):
            kw = {}
            if bias is not None:
                kw["bias"] = bias
            if scale is not None:
                kw["scale"] = scale
            P.op("act", lambda e: e.activation(out=out, in_=in_, func=func, **kw), reads=reads, writes=writes)

        def dve(fn, reads, writes):
            P.op("dve", fn, reads=reads, writes=writes)

        def col(t, c):
            return t[:, c:c + 1]

        P.op("sp", lambda e: e.dma_start(out=pp[:], in_=pp_d[:, :]), writes=[R_PP], dsem=s_pp)
        dve(lambda e: e.memset(ones[:], 1.0), [], [R_ONES])
        dve(lambda e: e.memset(car_h[:], 0.0), [], R_CH)
        dve(lambda e: e.memset(car_u[:], 0.0), [], R_CU)
        act(dp[:, P_E:P_E + 8], pp[:, C_LAM:C_LAM + 8], AF.Exp, [R_PP], [R_DP], scale=-1.0)
        act(dp[:, P_L:P_L + 8], dp[:, P_E:P_E + 8], AF.Ln, [R_DP], [R_DP], bias=1.0)
        dve(lambda e: e.tensor_scalar(out=dp[:, P_CH:P_CH + 8], in0=dp[:, P_L:P_L + 8], scalar1=-4.0, scalar2=None,
                                      op0=ALU.mult), [R_DP], [R_DP])
        dve(lambda e: e.tensor_scalar(out=dp[:, P_C:P_C + 8], in0=dp[:, P_L:P_L + 8], scalar1=-8.0, scalar2=None,
                                      op0=ALU.mult), [R_DP], [R_DP])
        dve(lambda e: e.tensor_scalar(out=dp[:, P_HBA:P_HBA + 16], in0=pp[:, C_BA:C_BA + 16], scalar1=0.5, scalar2=None,
                                      op0=ALU.mult), [R_PP], [R_DP])
        dve(lambda e: e.tensor_tensor(out=dp[:, P_BPS:P_BPS + 8], in0=pp[:, C_BPOOL:C_BPOOL + 8],
                                      in1=pp[:, C_PSCALE:C_PSCALE + 8], op=ALU.mult), [R_PP], [R_DP])

        kv_tasks = []

        def kv_k(cb):
            def pe():
                s = load_block(w_k, 0, cb * BW)

                def fnk(e, s=s):
                    ins = None
                    for h in range(2):
                        for k in range(KC):
                            ins = e.matmul(PD[1][:, h * 256:(h + 1) * 256], lhsT=ring[:, s, k, h * 128:(h + 1) * 128],
                                           rhs=memv(k, 0, 256), start=(k == 0), stop=(k == KC - 1))
                    return ins
                P.op("pe", fnk, reads=[R_RING[s], R_MEMT], writes=[R_PD[1]])

            def ev(on_dve=False):
                for h in range(2):
                    c = cb * 2 + h
                    src = PD[1][:, h * 256:(h + 1) * 256]
                    dve(lambda e, c=c, src=src: e.reduce_sum(out=ksum[:, c:c + 1], in_=src, axis=mybir.AxisListType.X),
                        [R_PD[1]], [R_KS])
                    dve(lambda e, c=c: e.tensor_scalar(out=ksum[:, c:c + 1], in0=ksum[:, c:c + 1], scalar1=1.0 / 256,
                                                       scalar2=None, op0=ALU.mult), [R_KS], [R_KS])
                    dve(lambda e, c=c, src=src: e.tensor_scalar(out=kT[:, c, :], in0=src, scalar1=ksum[:, c:c + 1],
                                                                scalar2=None, op0=ALU.subtract), [R_PD[1], R_KS], [R_KT])
            return pe, ev

        def kv_v(cb):
            def pe():
                s = load_block(w_v, 0, cb * BW)

                def fnv(e, s=s):
                    ins = None
                    for mh in range(2):
                        for k in range(KC):
                            ins = e.matmul(PD[1][:, mh * 256:(mh + 1) * 256], lhsT=memv(k, mh * 128, 128),
                                           rhs=ring[:, s, k, :], start=(k == 0), stop=(k == KC - 1))
                    return ins
                P.op("pe", fnv, reads=[R_RING[s], R_MEMT], writes=[R_PD[1]])

            def ev(on_dve=False):
                for mh in range(2):
                    if on_dve:
                        dve(lambda e, mh=mh: e.tensor_copy(out=vv[:, mh, cb * BW:(cb + 1) * BW],
                                                           in_=PD[1][:, mh * 256:(mh + 1) * 256]), [R_PD[1]], [R_V])
                    else:
                        act(vv[:, mh, cb * BW:(cb + 1) * BW], PD[1][:, mh * 256:(mh + 1) * 256], AF.Copy,
                            [R_PD[1]], [R_V])
            return pe, ev

        for cb in range(D // BW):
            kv_tasks.append(kv_k(cb))
        for cb in range(D // BW):
            kv_tasks.append(kv_v(cb))

        lru_slot = {}

        def lru_Ppe(idx, srcsel):
            j = idx % 8
            if j % 2 == 0:
                lru_slot["s"] = load_block(w_in, 0, 1024 + (j // 2) * BW)
            src, rsrc = srcsel(idx)
            proj(0, lru_slot["s"], (j % 2) * 128, src, rsrc)

        def lru_Pact(idx):
            j, ui = idx % 8, idx % 3
            dve(lambda e: e.tensor_copy(out=sv("u", ui, 12, 4), in_=car_u[:, j, :]), [R_CU[j]], [R_SV["u"][ui]])
            act(sv("u", ui, 16, T), PD[0][:, :], AF.Copy, [R_PD[0]], [R_SV["u"][ui]])

        def lru_Ca(idx, tap0_dve=False):
            j = idx % 8
            ui, xi = idx % 3, idx % 3
            RU, RX = R_SV["u"][ui], R_SV["xc"][xi]
            cw = C_CONVW + 4 * j
            if tap0_dve:
                dve(lambda e: e.tensor_scalar(out=sv("xc", xi, 0, T), in0=sv("u", ui, 13, T), scalar1=col(pp, cw),
                                              scalar2=col(pp, C_CONVB + j), op0=ALU.mult, op1=ALU.add),
                    [RU, R_PP], [RX])
            else:
                act(sv("xc", xi, 0, T), sv("u", ui, 13, T), AF.Identity, [RU, R_PP], [RX],
                    bias=col(pp, C_CONVB + j), scale=col(pp, cw))
            for k in range(1, 4):
                dve(lambda e, k=k: e.scalar_tensor_tensor(out=sv("xc", xi, 0, T), in0=sv("u", ui, 13 + k, T),
                                                          scalar=col(pp, cw + k), in1=sv("xc", xi, 0, T),
                                                          op0=ALU.mult, op1=ALU.add),
                    [RU, RX, R_PP], [RX])
            dve(lambda e: e.tensor_copy(out=car_u[:, j, :], in_=sv("u", ui, 12 + T, 4)), [RU], [R_CU[j]])

        def lru_Cb(idx):
            j = idx % 8
            xi, bi = idx % 3, idx % 2
            RX = R_SV["xc"][xi]
            act(Bq[bi][:], sv("xc", xi, 0, T), AF.Copy, [RX], [R_B[bi]])
            for (dd, wt) in ((2, wa), (3, wx)):
                mm_big(dd, lambda k, wt=wt: wt[:, j, :], lambda k, tt: Bq[bi][:, tt * 512:(tt + 1) * 512], 1,
                       reads=[R_WS, R_B[bi]])

        def lru_A1(idx):
            j, ti = idx % 8, idx % 2
            RT, RI = R_SV["tha"][ti], R_SV["thi"][ti]
            act(sv("tha", ti, 0, T), PD[2][:, :], AF.Tanh, [R_PD[2], R_DP], [RT],
                bias=col(dp, P_HBA + j), scale=0.5)
            act(sv("thi", ti, 0, T), PD[3][:, :], AF.Tanh, [R_PD[3], R_DP], [RI],
                bias=col(dp, P_HBX + j), scale=0.5)

        def lru_A2(idx, reset):
            j, ti = idx % 8, idx % 2
            RT, RI, RA = R_SV["tha"][ti], R_SV["thi"][ti], R_SV["a"][ti]
            act(sv("a", ti, 0, T), sv("tha", ti, 0, T), AF.Exp, [RT, R_DP], [RA],
                bias=col(dp, P_CH + j), scale=col(dp, P_CH + j))
            act(sv("tha", ti, 0, T), sv("tha", ti, 0, T), AF.Exp, [RT, R_DP], [RT],
                bias=col(dp, P_C + j), scale=col(dp, P_C + j))
            act(sv("tha", ti, 0, T), sv("tha", ti, 0, T), AF.Sqrt, [RT], [RT], bias=1.0, scale=-1.0)
            if reset == "always":
                dve(lambda e: e.memset(sv("tha", ti, 0, 1), 1.0), [], [RT])
            elif reset == "flag":
                dve(lambda e: e.tensor_scalar(out=sv("tha", ti, 0, 1), in0=sv("tha", ti, 0, 1), scalar1=col(pp, C_FLAG),
                                              scalar2=col(pp, C_OMF), op0=ALU.mult, op1=ALU.add),
                    [RT, R_PP], [RT])

        def lru_B(idx, main):
            j, ti, xi = idx % 8, idx % 2, idx % 3
            RT, RI, RA, RX = R_SV["tha"][ti], R_SV["thi"][ti], R_SV["a"][ti], R_SV["xc"][xi]
            dve(lambda e: e.scalar_tensor_tensor(out=sv("xc", xi, 0, T), in0=sv("thi", ti, 0, T), scalar=1.0,
                                                 in1=sv("xc", xi, 0, T), op0=ALU.add, op1=ALU.mult),
                [RI, RX], [RX])
            dve(lambda e: e.scalar_tensor_tensor(out=sv("xc", xi, 0, T), in0=sv("xc", xi, 0, T), scalar=0.5,
                                                 in1=sv("tha", ti, 0, T), op0=ALU.mult, op1=ALU.mult),
                [RX, RT], [RX])
            dve(lambda e: e.tensor_tensor_scan(out=sv("tha", ti, 0, T), data0=sv("a", ti, 0, T), data1=sv("xc", xi, 0, T),
                                               initial=car_h[:, j:j + 1], op0=ALU.mult, op1=ALU.add),
                [RA, RX, R_CH[j], RT], [RT])
            dve(lambda e: e.tensor_copy(out=car_h[:, j:j + 1], in_=sv("tha", ti, T - 1, 1)), [RT], [R_CH[j]])
            if main:
                dve(lambda e: e.tensor_tensor(out=tb[:, 8 + j, :], in0=sv("tha", ti, 0, T), in1=tb[:, 8 + j, :],
                                              op=ALU.mult), [RT, R_TB[8 + j]], [R_TB[8 + j]])

        def lru_loop(n, srcsel, main, reset_fn, kv=False, hook=None, side=None):
            for step in range(-2, n):
                if hook is not None:
                    hook(step)
                if side is not None:
                    side[0](step)
                if 0 <= step < n:
                    lru_A1(step)
                if 0 <= step + 2 < n:
                    lru_Ppe(step + 2, srcsel)
                if side is not None:
                    side[1](step)
                if 0 <= step + 1 < n:
                    lru_Ca(step + 1, tap0_dve=not main)
                if 0 <= step < n:
                    lru_A2(step, reset_fn(step))
                if 0 <= step + 1 < n:
                    lru_Cb(step + 1)
                if 0 <= step + 2 < n:
                    lru_Pact(step + 2)
                if 0 <= step < n:
                    lru_B(step, main)
                if side is not None:
                    side[2](step)
                if kv and step >= 0:
                    if kv_state["pending"] is not None:
                        kv_state["pending"]()
                        kv_state["pending"] = None
                    if kv_tasks:
                        pe, ev = kv_tasks.pop(0)
                        pe()
                        kv_state["pending"] = ev
            state["d"] = 0

        kv_state = {"pending": None}

        def ln_stats_chunk(c, last):
            d1, d2 = 2, 3
            b = R_B[c % 2]
            act(Bq[c % 2][:], xfc(c), AF.Square, [R_XF[c]], [b])
            if last:
                zb, rzb = GB[2 + c % 2][:, 0:T], R_G[2 + c % 2]
            else:
                zb, rzb = xb[:, c, :], R_XB[c]
            dve(lambda e, c=c, zb=zb: e.tensor_copy(out=zb, in_=xfc(c)), [R_XF[c]], [rzb])

            def fn(e, c=c, zb=zb):
                ins = None
                for tt in range(2):
                    e.matmul(PD[d1][:, tt * 512:(tt + 1) * 512], lhsT=ones[:], rhs=zb[:, tt * 512:(tt + 1) * 512],
                             start=(c == 0), stop=(c == KC - 1))
                    ins = e.matmul(PD[d2][:, tt * 512:(tt + 1) * 512], lhsT=ones[:],
                                   rhs=Bq[c % 2][:, tt * 512:(tt + 1) * 512], start=(c == 0), stop=(c == KC - 1))
                return ins
            P.op("pe", fn, reads=[R_ONES, rzb, b], writes=[R_PD[d1], R_PD[d2]])

        def layer_norm(li, last, t0, pipe=None, side=None):
            d1, d2 = 2, 3
            inv = 1.0 / D
            dve(lambda e: e.tensor_scalar(out=G[0][:], in0=PD[d1][:, :], scalar1=inv, scalar2=None, op0=ALU.mult),
                [R_PD[d1]], [R_G[0]])
            dve(lambda e: e.tensor_tensor(out=G[2][:], in0=G[0][:], in1=G[0][:], op=ALU.mult), [R_G[0]], [R_G[2]])
            dve(lambda e: e.scalar_tensor_tensor(out=G[1][:], in0=PD[d2][:, :], scalar=inv, in1=G[2][:],
                                                 op0=ALU.mult, op1=ALU.subtract), [R_PD[d2], R_G[2]], [R_G[1]])
            act(G[1][:], G[1][:], AF.Ln, [R_G[1]], [R_G[1]], bias=LN_EPS)
            act(G[1][:], G[1][:], AF.Exp, [R_G[1]], [R_G[1]], scale=-0.5)
            dve(lambda e: e.scalar_tensor_tensor(out=G[0][:], in0=G[0][:], scalar=-1.0, in1=G[1][:],
                                                 op0=ALU.mult, op1=ALU.mult), [R_G[0], R_G[1]], [R_G[0]])
            state["dset"] = [0, 1, 2, 3]
            cg = C_LN + 32 * li
            NG = 4
            if pipe is not None:
                pw, pcol, pevac = pipe
                pslots = [load_block(pw, 0, pcol), load_block(pw, 0, pcol + BW)]
            for c in range(KC):
                ti = 2 + (c % 2)
                dve(lambda e, c=c, ti=ti: e.tensor_tensor(out=G[ti][:], in0=xfc(c), in1=G[1][:], op=ALU.mult),
                    [R_XF[c], R_G[1]], [R_G[ti]])
                dve(lambda e, ti=ti: e.tensor_tensor(out=G[ti][:], in0=G[ti][:], in1=G[0][:], op=ALU.add),
                    [R_G[ti], R_G[0]], [R_G[ti]])
                act(xfc(c), G[ti][:], AF.Identity, [R_G[ti], R_PP], [R_XF[c]],
                    bias=col(pp, cg + 16 + c), scale=col(pp, cg + c))
                if not last:
                    act(xb[:, c, :], G[ti][:], AF.Identity, [R_G[ti], R_PP], [R_XB[c]],
                        bias=col(pp, cg + 16 + c), scale=col(pp, cg + c))
                    if pipe is not None:
                        def fnp(e, c=c):
                            ins = None
                            for g in range(NG):
                                for tt in range(2):
                                    ins = e.matmul(PD[g][:, tt * 512:(tt + 1) * 512],
                                                   lhsT=ring[:, pslots[g // 2], c, (g % 2) * 128:(g % 2) * 128 + 128],
                                                   rhs=xb[:, c, tt * 512:(tt + 1) * 512],
                                                   start=(c == 0), stop=(c == KC - 1))
                            return ins
                        P.op("pe", fnp, reads=[R_XB[c], R_RING[pslots[0]], R_RING[pslots[1]]], writes=R_PD)
                else:
                    dst = outT[c * 128:(c + 1) * 128, t0:t0 + T]
                    P.op("sp", lambda e, c=c, dst=dst: e.dma_start(out=dst, in_=xfc(c)), reads=[R_XF[c]], dsem=s_out)
                if side and c % 2 == 1:
                    side.pop(0)()
            if last:
                for c in range(KC):
                    R_XF[c].r[s_out] = s_out.n
            if pipe is not None:
                for g in range(NG):
                    pevac(g, g)
                state["d"] = 3

        def residual(c, di, first=True):
            if first:
                dve(lambda e: e.scalar_tensor_tensor(out=xfc(c), in0=xfc(c), scalar=ALPHA, in1=PD[di][:, :],
                                                     op0=ALU.mult, op1=ALU.add), [R_XF[c], R_PD[di]], [R_XF[c]])
            else:
                dve(lambda e: e.tensor_tensor(out=xfc(c), in0=xfc(c), in1=PD[di][:, :], op=ALU.add),
                    [R_XF[c], R_PD[di]], [R_XF[c]])

        def dense_to_resid(w_ap, r0, first=True, ln_last=None):
            LAG = 2
            if ln_last is not None:
                state["dset"] = [0, 1]
                state["d"] = 0
            slot = None
            for c in range(KC):
                if c % 2 == 0:
                    slot = load_block(w_ap, r0, (c // 2) * BW)
                di = nextD()
                proj(di, slot, (c % 2) * 128, tb, R_TB)
                residual(c, di, first)
                if ln_last is not None and c >= LAG:
                    ln_stats_chunk(c - LAG, ln_last)
            if ln_last is not None:
                for c in range(KC - LAG, KC):
                    ln_stats_chunk(c, ln_last)

        def pre_src(idx):
            return (xb, R_XB) if idx < 8 else (tb, R_TB)

        main_x_loaded = {"done": False}

        def load_main_x(p):
            src = xT[:, p * T:(p + 1) * T].rearrange("(k p) t -> p k t", p=128)
            P.op("pool", lambda e, src=src: e.dma_start(out=xb[:], in_=src), writes=R_XB, dsem=s_xb)

        def late_setup_loads():
            P.op("pool", lambda e: e.dma_start(out=wa[:], in_=w_a.rearrange("h i j -> i h j")), writes=[R_WS], dsem=s_ws)
            P.op("pool", lambda e: e.dma_start(out=wx[:], in_=w_x.rearrange("h i j -> i h j")), writes=[R_WS], dsem=s_ws)
            P.op("pool", lambda e: e.dma_start(out=wp[:], in_=w_pool.rearrange("g (k p) n -> p g k n", p=128)),
                 writes=[R_WS], dsem=s_ws)
            for hh in range(2):
                P.op("pool", lambda e, hh=hh: e.dma_start(
                    out=GB[hh][:, :].rearrange("p (k n) -> p k n", n=256),
                    in_=memT[hh * 1024:(hh + 1) * 1024, :].rearrange("(k p) n -> p k n", p=128)),
                    writes=[R_MEMT], dsem=s_mem)

        def pre_hook(step):
            if step == -1:
                late_setup_loads()
            if step == -2:
                P.op("pool", lambda e: e.dma_start(out=xhb[:], in_=xhT.rearrange("(k p) t -> p k t", p=128)),
                     writes=[R_XH], dsem=s_xh)
            if 0 <= step < 6:
                for k in range(3 * step, min(3 * step + 3, KC)):
                    srck = xpT[k * 128:(k + 1) * 128, T:2 * T]
                    P.op("pool", lambda e, k=k, srck=srck: e.dma_start(out=tb[:, k, :], in_=srck),
                         writes=[R_TB[k]], dsem=s_tbx)
                if step == 5:
                    for k in range(KC):
                        R_TB[k].w = (s_tbx, s_tbx.n)
            if 6 <= step < 14:
                for k in (2 * (step - 6), 2 * (step - 6) + 1):
                    srck = xT[k * 128:(k + 1) * 128, 0:T]
                    P.op("pool", lambda e, k=k, srck=srck: e.dma_start(out=xb[:, k, :], in_=srck),
                         writes=[R_XB[k]], dsem=s_xb)
                if step == 13:
                    for k in range(KC):
                        R_XB[k].w = (s_xb, s_xb.n)

        src0 = xpT[:, 0:T].rearrange("(k p) t -> p k t", p=128)
        P.op("pool", lambda e: e.dma_start(out=xb[:], in_=src0), writes=R_XB, dsem=s_xb)
        lru_loop(16, pre_src, False, lambda i: "always" if i < 8 else None, kv=True, hook=pre_hook)
        if kv_state["pending"] is not None:
            kv_state["pending"]()
            kv_state["pending"] = None
        dve(lambda e: e.tensor_scalar(out=car_h[:], in0=car_h[:], scalar1=col(pp, C_FLAG), scalar2=None, op0=ALU.mult),
            R_CH + [R_PP], R_CH)

        gate_done = {}
        gate_slot = {}

        def make_gate_tasks():
            def mk(j):
                def task():
                    if j % 2 == 0:
                        gate_slot["s"] = load_block(w_in, 0, 2048 + (j // 2) * BW)
                    di = nextD()
                    proj(di, gate_slot["s"], (j % 2) * 128, xb, R_XB)
                    act(tb[:, 8 + j, :], PD[di][:, :], AF.Gelu_apprx_tanh, [R_PD[di]], [R_TB[8 + j]])
                return task
            return [mk(j) for j in range(8)]

        for p in range(NPASS):
            t0 = p * T
            P.inherit(R_XF, ALL_SCR)
            src = xT[:, t0:t0 + T].rearrange("(k p) t -> p k t", p=128)
            if not gate_done.get(p, False):
                for task in make_gate_tasks():
                    task()
            pool_slot = {}

            def mbv(cc):
                return tb[:, cc, :], R_TB[cc]

            def pool_halo(g, h):
                if h == 0:
                    pool_slot["s"] = load_block(w_in, 0, g * BW)
                slot = pool_slot["s"]
                cc = 2 * g + h
                if p == 0:
                    mm_big(1, lambda k, slot=slot, h=h: ring[:, slot, k, h * 128:(h + 1) * 128],
                           lambda k, tt: xhb[:, k, :], KC, reads=[R_RING[slot], R_XH], ntile=1, n=16)
                    act(car_p[:, cc, :], PD[1][:, 0:16], AF.Copy, [R_PD[1]], [R_CP[cc]])

            def pool_proj(g, h):
                proj(1, pool_slot["s"], h * 128, xb, R_XB)

            def pool_rest(g, h):
                w = 2 ** (g + 1)
                cc = 2 * g + h
                RUP = R_SV["up"][h]
                RQ, RR = R_SV["Q"][0], R_SV["R"][0]
                dve(lambda e, cc=cc, h=h: e.tensor_copy(out=sv("up", h, 0, 16), in_=car_p[:, cc, :]), [R_CP[cc]], [RUP])
                act(sv("up", h, 16, T), PD[1][:, :], AF.Copy, [R_PD[1]], [RUP])
                dve(lambda e, cc=cc, h=h: e.tensor_copy(out=car_p[:, cc, :], in_=sv("up", h, T, 16)), [RUP], [R_CP[cc]])
                srcn, srci, srcr = "up", h, RUP
                sh = 1
                for st in range(g + 1):
                    dstn, dstr = ("Q", RQ) if st % 2 == 0 else ("R", RR)
                    lo = 2 * sh - 1
                    dve(lambda e, srcn=srcn, srci=srci, dstn=dstn, lo=lo, sh=sh: e.tensor_tensor(
                        out=sv(dstn, 0, lo, 1040 - lo), in0=sv(srcn, srci, lo, 1040 - lo),
                        in1=sv(srcn, srci, lo - sh, 1040 - lo), op=ALU.add), [srcr], [dstr])
                    srcn, srci, srcr = dstn, 0, dstr
                    sh *= 2
                mv, mr = mbv(cc)
                dve(lambda e, srcn=srcn, srci=srci, mv=mv, w=w, h=h: e.scalar_tensor_tensor(
                    out=mv, in0=sv(srcn, srci, 16, T), scalar=1.0 / w, in1=sv("up", h, 16, T),
                    op0=ALU.mult, op1=ALU.subtract), [srcr, RUP], [mr])
                if p == 0:
                    dve(lambda e, srcn=srcn, srci=srci, g=g: e.tensor_tensor(
                        out=tmp16[:], in0=sv(srcn, srci, 16, 16), in1=pp[:, C_RC + 16 * g:C_RC + 16 * g + 16],
                        op=ALU.mult), [srcr, R_PP], [R_T16])
                    dve(lambda e, mv=mv, h=h: e.tensor_tensor(out=mv[:, 0:16], in0=tmp16[:], in1=sv("up", h, 16, 16),
                                                              op=ALU.subtract), [R_T16, RUP], [mr])

            def pool_B(g):
                dis = []
                for h in range(2):
                    di = nextD()
                    dis.append(di)
                    mm_big(di, lambda k, g=g, h=h: wp[:, g, k, h * 128:(h + 1) * 128],
                           lambda k, tt, g=g: tb[:, 2 * g + k, tt * 512:(tt + 1) * 512], 2,
                           reads=[R_WS, R_TB[2 * g], R_TB[2 * g + 1]])
                for h in range(2):
                    oc = 2 * g + h
                    act(tb[:, oc, :], PD[dis[h]][:, :], AF.Identity, [R_PD[dis[h]], R_PP, R_DP], [R_TB[oc]],
                        bias=col(dp, P_BPS + oc), scale=col(pp, C_PSCALE + oc))

            def side0(step):
                if kv_tasks:
                    pe, ev = kv_tasks.pop(0)
                    pe()
                    ev(on_dve=False)
                cc = step + 2
                if 0 <= cc < 8:
                    pool_halo(cc // 2, cc % 2)

            def side1(step):
                cc = step + 2
                if 0 <= cc < 8:
                    pool_proj(cc // 2, cc % 2)

            def side2(step):
                cc = step + 2
                if 0 <= cc < 8:
                    pool_rest(cc // 2, cc % 2)

            lru_loop(8, lambda i: (xb, R_XB), True, (lambda i: "flag") if p == 0 else (lambda i: None),
                     side=(side0, side1, side2))
            for g in range(4):
                pool_B(g)
            if p == 0:
                assert not kv_tasks
                P.inherit([R_MEMT], R_G[0:2])
            P.inherit(ALL_SCR, R_XF)
            for c in range(KC):
                srcc = xT[c * 128:(c + 1) * 128, t0:t0 + T]
                P.op("sp", lambda e, c=c, srcc=srcc: e.dma_start(out=xfc(c), in_=srcc), writes=[R_XF[c]], dsem=s_xfc[c])
            dense_to_resid(w_out, 0, ln_last=False)
            def q_evac(c, di):
                act(tb[:, c, :], PD[di][:, :], AF.Copy, [R_PD[di]], [R_TB[c]])

            layer_norm(0, False, t0, pipe=(w_q, 0, q_evac))

            slot = None
            for c in range(4, KC):
                if c % 2 == 0:
                    slot = load_block(w_q, 0, (c // 2) * BW)
                di = nextD()
                proj(di, slot, (c % 2) * 128, xb, R_XB)
                q_evac(c, di)
            sc = 512.0 ** -0.5

            def ebuf(hd, mh):
                if hd % 2 == 0:
                    return Bq[mh][:, :], R_B[mh]
                return GB[2][:, mh * T:(mh + 1) * T], R_G[2]

            def att_S(hd):
                for mh in range(2):
                    eb, er = ebuf(hd, mh)
                    di = nextD()
                    mm_big(di, lambda k, hd=hd, mh=mh: kT[:, 4 * hd + k, mh * 128:(mh + 1) * 128],
                           lambda k, tt, hd=hd: tb[:, 4 * hd + k, tt * 512:(tt + 1) * 512], 4,
                           reads=[R_KT] + R_TB[4 * hd:4 * hd + 4])
                    act(eb, PD[di][:, :], AF.Exp, [R_PD[di]], [er], scale=sc)

            def att_V(hd):
                e0, r0 = ebuf(hd, 0)
                e1, r1 = ebuf(hd, 1)
                ebs = (e0, e1)
                di = nextD()
                mm_big(di, lambda k: ones[:], lambda k, tt: ebs[k][:, tt * 512:(tt + 1) * 512], 2,
                       reads=[R_ONES, r0, r1])
                gi = hd % 2
                act(G[gi][:], PD[di][:, :], AF.Ln, [R_PD[di]], [R_G[gi]])
                act(G[gi][:], G[gi][:], AF.Exp, [R_G[gi]], [R_G[gi]], scale=-1.0)
                for dc in range(4):
                    c = 4 * hd + dc
                    di = nextD()
                    mm_big(di, lambda k, c=c: vv[:, k, c * 128:(c + 1) * 128],
                           lambda k, tt: ebs[k][:, tt * 512:(tt + 1) * 512], 2, reads=[R_V, r0, r1])
                    dve(lambda e, c=c, di=di, gi=gi: e.tensor_tensor(out=tb[:, c, :], in0=PD[di][:, :], in1=G[gi][:],
                                                                     op=ALU.mult), [R_PD[di], R_G[gi]], [R_TB[c]])

            att_S(0)
            for hd in range(4):
                if hd + 1 < 4:
                    att_S(hd + 1)
                att_V(hd)
            dense_to_resid(w_o, 0, ln_last=False)
            def ff1_evac(j, di):
                gi = j % 2
                act(G[gi][:], PD[di][:, :], AF.Relu, [R_PD[di]], [R_G[gi]])
                dve(lambda e, j=j, gi=gi: e.tensor_tensor(out=tb[:, j, :], in0=G[gi][:], in1=G[gi][:], op=ALU.mult),
                    [R_G[gi]], [R_TB[j]])

            layer_norm(1, False, t0, pipe=(w_ff1, 0, ff1_evac))

            for r in range(4):
                slot = None
                for j in range(4 if r == 0 else 0, KC):
                    hc = 16 * r + j
                    if j % 2 == 0:
                        slot = load_block(w_ff1, 0, (hc // 2) * BW)
                    di = nextD()
                    proj(di, slot, (hc % 2) * 128, xb, R_XB)
                    ff1_evac(j, di)
                if r == 3 and p + 1 < NPASS:
                    load_main_x(p + 1)
                dense_to_resid(w_ff2, r * D, first=(r == 0), ln_last=(True if r == 3 else None))
            if p + 1 < NPASS:
                side = make_gate_tasks()
                layer_norm(2, True, t0, side=side)
                while side:
                    side.pop(0)()
                gate_done[p + 1] = True
            else:
                layer_norm(2, True, t0)

        def semof(k):
            return k.h if isinstance(k, DSem) else sem_cnt[k]

        def emit(name, e):
            for waits, fn, ev in P.q[name]:
                for k, v in waits:
                    e.wait_ge(semof(k), v)
                ins = fn(e)
                k, v = ev
                ins.then_inc(semof(k), 16 if isinstance(k, DSem) else 1)

        with nc.Block() as block:
            @block.tensor
            def _(e):
                emit("pe", e)

            @block.scalar
            def _(e):
                emit("act", e)

            @block.vector
            def _(e):
                emit("dve", e)

            @block.gpsimd
            def _(e):
                emit("pool", e)

            @block.sync
            def _(e):
                emit("sp", e)
                e.wait_ge(s_out.h, s_out.n)
    return nc


def _prep_inputs(inputs):
    f = lambda a: np.ascontiguousarray(np.asarray(a, dtype=np.float32))
    x = f(inputs["x"])
    mem = f(inputs["mem"])
    shared = {
        "w_in": f(inputs["w_in"][0]), "w_out": f(inputs["w_out"][0]), "w_q": f(inputs["w_q"][0]),
        "w_k": f(inputs["w_k"][0]), "w_v": f(inputs["w_v"][0]), "w_o": f(inputs["w_o"][0]),
        "w_ff1": f(inputs["w_ff1"][0]), "w_ff2": f(inputs["w_ff2"][0]),
        "w_a": f(inputs["w_a"][0]), "w_x": f(inputs["w_x"][0]), "w_pool": f(inputs["w_pool"][0]),
    }
    pp = np.zeros((128, NPP), np.float32)

    def cols(v, n):
        return np.asarray(v, np.float32).reshape(n, 128).T

    cw = np.asarray(inputs["conv_w"][0], np.float32)
    pp[:, C_CONVW:C_CONVW + 32] = cw.reshape(4, 8, 128).transpose(2, 1, 0).reshape(128, 32)
    pp[:, C_CONVB:C_CONVB + 8] = cols(inputs["conv_b"][0], 8)
    pp[:, C_BA:C_BA + 8] = cols(np.asarray(inputs["b_a"][0]).reshape(-1), 8)
    pp[:, C_BX:C_BX + 8] = cols(np.asarray(inputs["b_x"][0]).reshape(-1), 8)
    pp[:, C_LAM:C_LAM + 8] = cols(inputs["lru_lambda"][0], 8)
    pp[:, C_BPOOL:C_BPOOL + 8] = cols(np.asarray(inputs["b_pool"][0]).reshape(-1), 8)
    pp[:, C_PSCALE:C_PSCALE + 8] = cols(inputs["pool_scale"][0], 8)
    for i, nm in enumerate(("ln1_g", "ln1_b", "ln2_g", "ln2_b", "ln3_g", "ln3_b")):
        pp[:, C_LN + 16 * i:C_LN + 16 * (i + 1)] = cols(inputs[nm][0], 16)
    in_maps = []
    for core in range(NCORE):
        b, half = core // 2, core % 2
        m = dict(shared)
        m["xT"] = np.ascontiguousarray(x[b, half * TOK:(half + 1) * TOK, :].T)
        m["xpT"] = np.ascontiguousarray(x[b, 0:TOK, :].T) if half == 1 else np.zeros((D, TOK), np.float32)
        m["memT"] = np.ascontiguousarray(mem[b].T)
        m["xhT"] = np.ascontiguousarray(x[b, TOK - 16:TOK, :].T) if half == 1 else np.zeros((D, 16), np.float32)
        ppc = pp.copy()
        ppc[:, C_FLAG] = float(half)
        ppc[:, C_OMF] = 1.0 - float(half)
        for g in range(4):
            w = 2 ** (g + 1)
            for t in range(16):
                ppc[:, C_RC + 16 * g + t] = (1.0 / min(t + 1, w)) if half == 0 else (1.0 / w)
        m["pp"] = ppc
        in_maps.append(m)
    return in_maps


_NC_CACHE = {}


def kernel(**inputs):
    in_maps = _prep_inputs(inputs)
    if "nc" not in _NC_CACHE:
        _NC_CACHE["nc"] = build_nc()
    nc = _NC_CACHE["nc"]
    res = run_bass_kernel_spmd(nc, in_maps, core_ids=list(range(NCORE)))
    out = np.empty((NB, SEQ, D), np.float32)
    for core in range(NCORE):
        b, half = core // 2, core % 2
        out[b, half * TOK:(half + 1) * TOK, :] = res.results[core]["outT"].T
    return out
```

```python
import contextlib
import numpy as np
import concourse.bass as bass
import concourse.mybir as mybir
from concourse.bass_utils import run_bass_kernel_spmd

F32 = mybir.dt.float32
BF16 = mybir.dt.bfloat16
ALU = mybir.AluOpType
AF = mybir.ActivationFunctionType

D = 2048
SEQ = 4096
NB = 4
NCORE = 8
TOK = 2048
T = 1024
NPASS = 2
KC = 16
ALPHA = 2.0 ** 0.25
LN_EPS = 1e-5
NPP = 256
BW = 256
NSLOT = 4

C_CONVW, C_CONVB, C_BA, C_BX, C_LAM, C_BPOOL, C_PSCALE = 0, 32, 40, 48, 56, 64, 72
C_LN = 80
C_FLAG, C_OMF, C_RC = 176, 177, 178
P_CH, P_C, P_HBA, P_HBX, P_BPS, P_E, P_L = 0, 8, 16, 24, 32, 40, 48


class Res:
    __slots__ = ("name", "w", "r")

    def __init__(self, name):
        self.name = name
        self.w = None
        self.r = {}


class DSem:
    def __init__(self, handle):
        self.h = handle
        self.n = 0


class Prog:
    ENG = ("pe", "act", "dve", "pool", "sp")

    def __init__(self):
        self.q = {e: [] for e in self.ENG}
        self.cnt = {e: 0 for e in self.ENG}
        self.waited = {e: {} for e in self.ENG}

    def op(self, eng, fn, reads=(), writes=(), dsem=None):
        raw = {}
        oth = {}

        def add(d, ev):
            if ev is None:
                return
            k, v = ev
            if d.get(k, 0) < v:
                d[k] = v

        for r in reads:
            add(raw, r.w)
        for w in writes:
            add(oth, w.w)
            for k, v in w.r.items():
                add(oth, (k, v))
        if dsem is not None:
            dsem.n += 16
            ev = (dsem, dsem.n)
        else:
            self.cnt[eng] += 1
            ev = (eng, self.cnt[eng])
        deps = dict(raw)
        for k, v in oth.items():
            if k == eng and dsem is None:
                continue
            if deps.get(k, 0) < v:
                deps[k] = v
        waits = []
        wd = self.waited[eng]
        for k, v in deps.items():
            if wd.get(k, 0) >= v:
                continue
            wd[k] = v
            waits.append((k, v))
        self.q[eng].append((waits, fn, ev))
        for r in reads:
            if r.r.get(ev[0], 0) < ev[1]:
                r.r[ev[0]] = ev[1]
        for w in writes:
            w.w = ev
            w.r = {}
        return ev

    @staticmethod
    def inherit(srcs, dsts):
        ev = {}
        for s in srcs:
            if s.w is not None and ev.get(s.w[0], 0) < s.w[1]:
                ev[s.w[0]] = s.w[1]
            for k, v in s.r.items():
                if ev.get(k, 0) < v:
                    ev[k] = v
        for d in dsts:
            for k, v in ev.items():
                if d.r.get(k, 0) < v:
                    d.r[k] = v


def build_nc(stop=None):
    nc = bass.Bass("TRN2", target_bir_lowering=False)
    P = Prog()

    def dram(name, shape, kind="ExternalInput"):
        return nc.dram_tensor(name, list(shape), F32, kind=kind).ap()

    xT = dram("xT", [D, TOK])
    xpT = dram("xpT", [D, TOK])
    memT = dram("memT", [D, 256])
    pp_d = dram("pp", [128, NPP])
    w_in = dram("w_in", [D, 3072])
    w_out = dram("w_out", [D, D])
    w_q = dram("w_q", [D, D])
    w_k = dram("w_k", [D, D])
    w_v = dram("w_v", [D, D])
    w_o = dram("w_o", [D, D])
    w_ff1 = dram("w_ff1", [D, 4 * D])
    w_ff2 = dram("w_ff2", [4 * D, D])
    w_a = dram("w_a", [8, 128, 128])
    w_x = dram("w_x", [8, 128, 128])
    w_pool = dram("w_pool", [4, 256, 256])
    xhT = dram("xhT", [D, 16])
    outT = dram("outT", [D, TOK], kind="ExternalOutput")

    es = contextlib.ExitStack()
    with es:
        def sb(name, shape, dt):
            return es.enter_context(nc.sbuf_tensor(name, list(shape), dt))

        xf = sb("xf", [128, KC * T + 128], F32)
        xb = sb("xb", [128, KC, T], BF16)
        tb = sb("tb", [128, KC, T], BF16)
        ring = sb("ring", [128, NSLOT, KC, BW], BF16)
        kT = sb("kT", [128, KC, 256], BF16)
        vv = sb("vv", [128, 2, D], BF16)
        wa = sb("wa", [128, 8, 128], BF16)
        wx = sb("wx", [128, 8, 128], BF16)
        wp = sb("wp", [128, 4, 2, 256], BF16)
        pp = sb("ppar", [128, NPP], F32)
        dp = sb("dpar", [128, 64], F32)
        ones = sb("ones", [128, 128], BF16)
        car_h = sb("car_h", [128, 8], F32)
        car_u = sb("car_u", [128, 8, 4], F32)
        car_p = sb("car_p", [128, 8, 16], F32)
        tmp16 = sb("tmp16", [128, 16], F32)
        ksum = sb("ksum", [128, 16], F32)
        G = [sb(f"G{i}", [128, T], F32) for i in range(4)]
        Bq = [sb(f"B{i}", [128, T], BF16) for i in range(2)]
        xhb = sb("xhb", [128, KC, 16], BF16)
        PD = [es.enter_context(nc.psum_tensor(f"pd{i}", [128, T], F32)) for i in range(4)]

        sem_cnt = {e: es.enter_context(nc.semaphore(f"c_{e}")) for e in ("pe", "act", "dve", "pool")}

        def dsem(name):
            return DSem(es.enter_context(nc.semaphore(name)))

        s_ring = [dsem(f"ring{i}") for i in range(NSLOT)]
        s_xb = dsem("xbld")
        s_xfc = [dsem(f"xfld{c}") for c in range(KC)]
        s_out = dsem("outst")
        s_pp = dsem("ppld")
        s_ws = dsem("wsld")
        s_mem = dsem("memld")
        s_xh = dsem("xhld")
        s_tbx = dsem("tbxld")

        R_XF = [Res(f"xf{c}") for c in range(KC)]
        R_XB = [Res(f"xb{c}") for c in range(KC)]
        R_TB = [Res(f"tb{c}") for c in range(KC)]
        R_RING = [Res(f"ring{i}") for i in range(NSLOT)]
        R_KT, R_V, R_WS, R_PP, R_DP, R_ONES = Res("kT"), Res("v"), Res("ws"), Res("pp"), Res("dp"), Res("ones")
        R_MEMT = Res("memT")
        R_CH = [Res(f"ch{j}") for j in range(8)]
        R_CU = [Res(f"cu{j}") for j in range(8)]
        R_CP = [Res(f"cp{j}") for j in range(8)]
        R_T16 = Res("t16")
        R_KS = Res("ksum")
        R_G = [Res(f"G{i}") for i in range(4)]
        R_B = [Res(f"B{i}") for i in range(2)]
        R_XH = Res("xh")
        R_PD = [Res(f"pd{i}") for i in range(4)]
        VOFF = {"u": (0, 1040, 3), "xc": (3120, 1024, 3), "tha": (6192, 1024, 2), "a": (8240, 1024, 2),
                "thi": (10288, 1024, 2), "up": (12336, 1040, 2), "Q": (14416, 1040, 1), "R": (15456, 1040, 1)}
        R_SV = {n: [Res(f"{n}{i}") for i in range(VOFF[n][2])] for n in VOFF}
        ALL_SCR = [r for n in R_SV for r in R_SV[n]]

        def sv(name, i, lo, n):
            b = VOFF[name][0] + i * VOFF[name][1]
            return xf[:, b + lo:b + lo + n]

        def xfc(c):
            return xf[:, c * T:(c + 1) * T]

        GB = [g.bitcast(BF16) for g in G]

        def memv(k, lo, n):
            return GB[k // 8][:, (k % 8) * 256 + lo:(k % 8) * 256 + lo + n]

        state = {"d": 0, "slot": 0, "dset": [0, 1, 2, 3]}

        def nextD():
            ds = state["dset"]
            state["d"] = (state["d"] + 1) % len(ds)
            return ds[state["d"]]

        def load_block(w_ap, r0, c0):
            s = state["slot"]
            state["slot"] = (s + 1) % NSLOT
            src = w_ap[r0:r0 + D, c0:c0 + BW].rearrange("(k p) n -> p k n", p=128)
            P.op("pool", lambda e, s=s, src=src: e.dma_start(out=ring[:, s], in_=src),
                 writes=[R_RING[s]], dsem=s_ring[s])
            return s

        def mm_big(di, lhs_fn, rhs_fn, nk, reads, ntile=2, n=512):
            def fn(e):
                ins = None
                for k in range(nk):
                    l = lhs_fn(k)
                    for tt in range(ntile):
                        ins = e.matmul(PD[di][:, tt * 512:tt * 512 + n], lhsT=l, rhs=rhs_fn(k, tt),
                                       start=(k == 0), stop=(k == nk - 1))
                return ins
            P.op("pe", fn, reads=reads, writes=[R_PD[di]])

        def proj(di, slot, off, src, R_SRC):
            mm_big(di, lambda k: ring[:, slot, k, off:off + 128],
                   lambda k, tt: src[:, k, tt * 512:(tt + 1) * 512], KC,
                   reads=[R_RING[slot]] + R_SRC)

        def act(out, in_, func, reads, writes, bias=None, scale=None):
            kw = {}
            if bias is not None:
                kw["bias"] = bias
            if scale is not None:
                kw["scale"] = scale
            P.op("act", lambda e: e.activation(out=out, in_=in_, func=func, **kw), reads=reads, writes=writes)

        def dve(fn, reads, writes):
            P.op("dve", fn, reads=reads, writes=writes)

        def col(t, c):
            return t[:, c:c + 1]

        P.op("sp", lambda e: e.dma_start(out=pp[:], in_=pp_d[:, :]), writes=[R_PP], dsem=s_pp)
        dve(lambda e: e.memset(ones[:], 1.0), [], [R_ONES])
        dve(lambda e: e.memset(car_h[:], 0.0), [], R_CH)
        dve(lambda e: e.memset(car_u[:], 0.0), [], R_CU)
        act(dp[:, P_E:P_E + 8], pp[:, C_LAM:C_LAM + 8], AF.Exp, [R_PP], [R_DP], scale=-1.0)
        act(dp[:, P_L:P_L + 8], dp[:, P_E:P_E + 8], AF.Ln, [R_DP], [R_DP], bias=1.0)
        dve(lambda e: e.tensor_scalar(out=dp[:, P_CH:P_CH + 8], in0=dp[:, P_L:P_L + 8], scalar1=-4.0, scalar2=None,
                                      op0=ALU.mult), [R_DP], [R_DP])
        dve(lambda e: e.tensor_scalar(out=dp[:, P_C:P_C + 8], in0=dp[:, P_L:P_L + 8], scalar1=-8.0, scalar2=None,
                                      op0=ALU.mult), [R_DP], [R_DP])
        dve(lambda e: e.tensor_scalar(out=dp[:, P_HBA:P_HBA + 16], in0=pp[:, C_BA:C_BA + 16], scalar1=0.5, scalar2=None,
                                      op0=ALU.mult), [R_PP], [R_DP])
        dve(lambda e: e.tensor_tensor(out=dp[:, P_BPS:P_BPS + 8], in0=pp[:, C_BPOOL:C_BPOOL + 8],
                                      in1=pp[:, C_PSCALE:C_PSCALE + 8], op=ALU.mult), [R_PP], [R_DP])

        kv_tasks = []

        def kv_k(cb):
            def pe():
                s = load_block(w_k, 0, cb * BW)

                def fnk(e, s=s):
                    ins = None
                    for h in range(2):
                        for k in range(KC):
                            ins = e.matmul(PD[1][:, h * 256:(h + 1) * 256], lhsT=ring[:, s, k, h * 128:(h + 1) * 128],
                                           rhs=memv(k, 0, 256), start=(k == 0), stop=(k == KC - 1))
                    return ins
                P.op("pe", fnk, reads=[R_RING[s], R_MEMT], writes=[R_PD[1]])

            def ev(on_dve=False):
                for h in range(2):
                    c = cb * 2 + h
                    src = PD[1][:, h * 256:(h + 1) * 256]
                    dve(lambda e, c=c, src=src: e.reduce_sum(out=ksum[:, c:c + 1], in_=src, axis=mybir.AxisListType.X),
                        [R_PD[1]], [R_KS])
                    dve(lambda e, c=c: e.tensor_scalar(out=ksum[:, c:c + 1], in0=ksum[:, c:c + 1], scalar1=1.0 / 256,
                                                       scalar2=None, op0=ALU.mult), [R_KS], [R_KS])
                    dve(lambda e, c=c, src=src: e.tensor_scalar(out=kT[:, c, :], in0=src, scalar1=ksum[:, c:c + 1],
                                                                scalar2=None, op0=ALU.subtract), [R_PD[1], R_KS], [R_KT])
            return pe, ev

        def kv_v(cb):
            def pe():
                s = load_block(w_v, 0, cb * BW)

                def fnv(e, s=s):
                    ins = None
                    for mh in range(2):
                        for k in range(KC):
                            ins = e.matmul(PD[1][:, mh * 256:(mh + 1) * 256], lhsT=memv(k, mh * 128, 128),
                                           rhs=ring[:, s, k, :], start=(k == 0), stop=(k == KC - 1))
                    return ins
                P.op("pe", fnv, reads=[R_RING[s], R_MEMT], writes=[R_PD[1]])

            def ev(on_dve=False):
                for mh in range(2):
                    if on_dve:
                        dve(lambda e, mh=mh: e.tensor_copy(out=vv[:, mh, cb * BW:(cb + 1) * BW],
                                                           in_=PD[1][:, mh * 256:(mh + 1) * 256]), [R_PD[1]], [R_V])
                    else:
                        act(vv[:, mh, cb * BW:(cb + 1) * BW], PD[1][:, mh * 256:(mh + 1) * 256], AF.Copy,
                            [R_PD[1]], [R_V])
            return pe, ev

        for cb in range(D // BW):
            kv_tasks.append(kv_k(cb))
        for cb in range(D // BW):
            kv_tasks.append(kv_v(cb))

        lru_slot = {}

        def lru_Ppe(idx, srcsel):
            j = idx % 8
            if j % 2 == 0:
                lru_slot["s"] = load_block(w_in, 0, 1024 + (j // 2) * BW)
            src, rsrc = srcsel(idx)
            proj(0, lru_slot["s"], (j % 2) * 128, src, rsrc)

        def lru_Pact(idx):
            j, ui = idx % 8, idx % 3
            dve(lambda e: e.tensor_copy(out=sv("u", ui, 12, 4), in_=car_u[:, j, :]), [R_CU[j]], [R_SV["u"][ui]])
            act(sv("u", ui, 16, T), PD[0][:, :], AF.Copy, [R_PD[0]], [R_SV["u"][ui]])

        def lru_Ca(idx, tap0_dve=False):
            j = idx % 8
            ui, xi = idx % 3, idx % 3
            RU, RX = R_SV["u"][ui], R_SV["xc"][xi]
            cw = C_CONVW + 4 * j
            if tap0_dve:
                dve(lambda e: e.tensor_scalar(out=sv("xc", xi, 0, T), in0=sv("u", ui, 13, T), scalar1=col(pp, cw),
                                              scalar2=col(pp, C_CONVB + j), op0=ALU.mult, op1=ALU.add),
                    [RU, R_PP], [RX])
            else:
                act(sv("xc", xi, 0, T), sv("u", ui, 13, T), AF.Identity, [RU, R_PP], [RX],
                    bias=col(pp, C_CONVB + j), scale=col(pp, cw))
            for k in range(1, 4):
                dve(lambda e, k=k: e.scalar_tensor_tensor(out=sv("xc", xi, 0, T), in0=sv("u", ui, 13 + k, T),
                                                          scalar=col(pp, cw + k), in1=sv("xc", xi, 0, T),
                                                          op0=ALU.mult, op1=ALU.add),
                    [RU, RX, R_PP], [RX])
            dve(lambda e: e.tensor_copy(out=car_u[:, j, :], in_=sv("u", ui, 12 + T, 4)), [RU], [R_CU[j]])

        def lru_Cb(idx):
            j = idx % 8
            xi, bi = idx % 3, idx % 2
            RX = R_SV["xc"][xi]
            act(Bq[bi][:], sv("xc", xi, 0, T), AF.Copy, [RX], [R_B[bi]])
            for (dd, wt) in ((2, wa), (3, wx)):
                mm_big(dd, lambda k, wt=wt: wt[:, j, :], lambda k, tt: Bq[bi][:, tt * 512:(tt + 1) * 512], 1,
                       reads=[R_WS, R_B[bi]])

        def lru_A1(idx):
            j, ti = idx % 8, idx % 2
            RT, RI = R_SV["tha"][ti], R_SV["thi"][ti]
            act(sv("tha", ti, 0, T), PD[2][:, :], AF.Tanh, [R_PD[2], R_DP], [RT],
                bias=col(dp, P_HBA + j), scale=0.5)
            act(sv("thi", ti, 0, T), PD[3][:, :], AF.Tanh, [R_PD[3], R_DP], [RI],
                bias=col(dp, P_HBX + j), scale=0.5)

        def lru_A2(idx, reset):
            j, ti = idx % 8, idx % 2
            RT, RI, RA = R_SV["tha"][ti], R_SV["thi"][ti], R_SV["a"][ti]
            act(sv("a", ti, 0, T), sv("tha", ti, 0, T), AF.Exp, [RT, R_DP], [RA],
                bias=col(dp, P_CH + j), scale=col(dp, P_CH + j))
            act(sv("tha", ti, 0, T), sv("tha", ti, 0, T), AF.Exp, [RT, R_DP], [RT],
                bias=col(dp, P_C + j), scale=col(dp, P_C + j))
            act(sv("tha", ti, 0, T), sv("tha", ti, 0, T), AF.Sqrt, [RT], [RT], bias=1.0, scale=-1.0)
            if reset == "always":
                dve(lambda e: e.memset(sv("tha", ti, 0, 1), 1.0), [], [RT])
            elif reset == "flag":
                dve(lambda e: e.tensor_scalar(out=sv("tha", ti, 0, 1), in0=sv("tha", ti, 0, 1), scalar1=col(pp, C_FLAG),
                                              scalar2=col(pp, C_OMF), op0=ALU.mult, op1=ALU.add),
                    [RT, R_PP], [RT])

        def lru_B(idx, main):
            j, ti, xi = idx % 8, idx % 2, idx % 3
            RT, RI, RA, RX = R_SV["tha"][ti], R_SV["thi"][ti], R_SV["a"][ti], R_SV["xc"][xi]
            dve(lambda e: e.scalar_tensor_tensor(out=sv("xc", xi, 0, T), in0=sv("thi", ti, 0, T), scalar=1.0,
                                                 in1=sv("xc", xi, 0, T), op0=ALU.add, op1=ALU.mult),
                [RI, RX], [RX])
            dve(lambda e: e.scalar_tensor_tensor(out=sv("xc", xi, 0, T), in0=sv("xc", xi, 0, T), scalar=0.5,
                                                 in1=sv("tha", ti, 0, T), op0=ALU.mult, op1=ALU.mult),
                [RX, RT], [RX])
            dve(lambda e: e.tensor_tensor_scan(out=sv("tha", ti, 0, T), data0=sv("a", ti, 0, T), data1=sv("xc", xi, 0, T),
                                               initial=car_h[:, j:j + 1], op0=ALU.mult, op1=ALU.add),
                [RA, RX, R_CH[j], RT], [RT])
            dve(lambda e: e.tensor_copy(out=car_h[:, j:j + 1], in_=sv("tha", ti, T - 1, 1)), [RT], [R_CH[j]])
            if main:
                dve(lambda e: e.tensor_tensor(out=tb[:, 8 + j, :], in0=sv("tha", ti, 0, T), in1=tb[:, 8 + j, :],
                                              op=ALU.mult), [RT, R_TB[8 + j]], [R_TB[8 + j]])

        def lru_loop(n, srcsel, main, reset_fn, kv=False, hook=None, side=None):
            for step in range(-2, n):
                if hook is not None:
                    hook(step)
                if side is not None:
                    side[0](step)
                if 0 <= step < n:
                    lru_A1(step)
                if 0 <= step + 2 < n:
                    lru_Ppe(step + 2, srcsel)
                if side is not None:
                    side[1](step)
                if 0 <= step + 1 < n:
                    lru_Ca(step + 1, tap0_dve=not main)
                if 0 <= step < n:
                    lru_A2(step, reset_fn(step))
                if 0 <= step + 1 < n:
                    lru_Cb(step + 1)
                if 0 <= step + 2 < n:
                    lru_Pact(step + 2)
                if 0 <= step < n:
                    lru_B(step, main)
                if side is not None:
                    side[2](step)
                if kv and step >= 0:
                    if kv_state["pending"] is not None:
                        kv_state["pending"]()
                        kv_state["pending"] = None
                    if kv_tasks:
                        pe, ev = kv_tasks.pop(0)
                        pe()
                        kv_state["pending"] = ev
            state["d"] = 0

        kv_state = {"pending": None}

        def ln_zb(c, last):
            if last:
                return GB[2 + c % 2][:, 0:T], R_G[2 + c % 2]
            return xb[:, c, :], R_XB[c]

        def ln_stats_prep(c, last):
            act(Bq[c % 2][:], xfc(c), AF.Square, [R_XF[c]], [R_B[c % 2]])
            zb, rzb = ln_zb(c, last)
            dve(lambda e, c=c, zb=zb: e.tensor_copy(out=zb, in_=xfc(c)), [R_XF[c]], [rzb])

        def ln_stats_pe(c, last):
            d1, d2 = 2, 3
            zb, rzb = ln_zb(c, last)

            def fn(e, c=c, zb=zb):
                ins = None
                for tt in range(2):
                    e.matmul(PD[d1][:, tt * 512:(tt + 1) * 512], lhsT=ones[:], rhs=zb[:, tt * 512:(tt + 1) * 512],
                             start=(c == 0), stop=(c == KC - 1))
                    ins = e.matmul(PD[d2][:, tt * 512:(tt + 1) * 512], lhsT=ones[:],
                                   rhs=Bq[c % 2][:, tt * 512:(tt + 1) * 512], start=(c == 0), stop=(c == KC - 1))
                return ins
            P.op("pe", fn, reads=[R_ONES, rzb, R_B[c % 2]], writes=[R_PD[d1], R_PD[d2]])

        def layer_norm(li, last, t0, pipe=None, side=None):
            d1, d2 = 2, 3
            inv = 1.0 / D
            dve(lambda e: e.tensor_scalar(out=G[0][:], in0=PD[d1][:, :], scalar1=inv, scalar2=None, op0=ALU.mult),
                [R_PD[d1]], [R_G[0]])
            dve(lambda e: e.tensor_tensor(out=G[2][:], in0=G[0][:], in1=G[0][:], op=ALU.mult), [R_G[0]], [R_G[2]])
            dve(lambda e: e.scalar_tensor_tensor(out=G[1][:], in0=PD[d2][:, :], scalar=inv, in1=G[2][:],
                                                 op0=ALU.mult, op1=ALU.subtract), [R_PD[d2], R_G[2]], [R_G[1]])
            act(G[1][:], G[1][:], AF.Ln, [R_G[1]], [R_G[1]], bias=LN_EPS)
            act(G[1][:], G[1][:], AF.Exp, [R_G[1]], [R_G[1]], scale=-0.5)
            dve(lambda e: e.scalar_tensor_tensor(out=G[0][:], in0=G[0][:], scalar=-1.0, in1=G[1][:],
                                                 op0=ALU.mult, op1=ALU.mult), [R_G[0], R_G[1]], [R_G[0]])
            state["dset"] = [0, 1, 2, 3]
            cg = C_LN + 32 * li
            NG = 4
            if pipe is not None:
                pw, pcol, pevac = pipe
                pslots = [load_block(pw, 0, pcol), load_block(pw, 0, pcol + BW)]
            for c in range(KC):
                ti = 2 + (c % 2)
                dve(lambda e, c=c, ti=ti: e.tensor_tensor(out=G[ti][:], in0=xfc(c), in1=G[1][:], op=ALU.mult),
                    [R_XF[c], R_G[1]], [R_G[ti]])
                dve(lambda e, ti=ti: e.tensor_tensor(out=G[ti][:], in0=G[ti][:], in1=G[0][:], op=ALU.add),
                    [R_G[ti], R_G[0]], [R_G[ti]])
                act(xfc(c), G[ti][:], AF.Identity, [R_G[ti], R_PP], [R_XF[c]],
                    bias=col(pp, cg + 16 + c), scale=col(pp, cg + c))
                if not last:
                    act(xb[:, c, :], G[ti][:], AF.Identity, [R_G[ti], R_PP], [R_XB[c]],
                        bias=col(pp, cg + 16 + c), scale=col(pp, cg + c))
                    if pipe is not None:
                        def fnp(e, c=c):
                            ins = None
                            for g in range(NG):
                                for tt in range(2):
                                    ins = e.matmul(PD[g][:, tt * 512:(tt + 1) * 512],
                                                   lhsT=ring[:, pslots[g // 2], c, (g % 2) * 128:(g % 2) * 128 + 128],
                                                   rhs=xb[:, c, tt * 512:(tt + 1) * 512],
                                                   start=(c == 0), stop=(c == KC - 1))
                            return ins
                        P.op("pe", fnp, reads=[R_XB[c], R_RING[pslots[0]], R_RING[pslots[1]]], writes=R_PD)
                else:
                    dst = outT[c * 128:(c + 1) * 128, t0:t0 + T]
                    P.op("sp", lambda e, c=c, dst=dst: e.dma_start(out=dst, in_=xfc(c)), reads=[R_XF[c]], dsem=s_out)
                if side and c % 2 == 1:
                    side.pop(0)()
            if last:
                for c in range(KC):
                    R_XF[c].r[s_out] = s_out.n
            if pipe is not None:
                for g in range(NG):
                    pevac(g, g)
                state["d"] = 3

        def residual(c, di, first=True):
            if first:
                dve(lambda e: e.scalar_tensor_tensor(out=xfc(c), in0=xfc(c), scalar=ALPHA, in1=PD[di][:, :],
                                                     op0=ALU.mult, op1=ALU.add), [R_XF[c], R_PD[di]], [R_XF[c]])
            else:
                dve(lambda e: e.tensor_tensor(out=xfc(c), in0=xfc(c), in1=PD[di][:, :], op=ALU.add),
                    [R_XF[c], R_PD[di]], [R_XF[c]])

        def dense_to_resid(w_ap, r0, first=True, ln_last=None):
            if ln_last is not None:
                state["dset"] = [0, 1]
                state["d"] = 0
            slot = None
            for c in range(KC):
                if c % 2 == 0:
                    slot = load_block(w_ap, r0, (c // 2) * BW)
                di = nextD()
                proj(di, slot, (c % 2) * 128, tb, R_TB)
                if ln_last is not None and c >= 1:
                    ln_stats_pe(c - 1, ln_last)
                residual(c, di, first)
                if ln_last is not None:
                    ln_stats_prep(c, ln_last)
            if ln_last is not None:
                ln_stats_pe(KC - 1, ln_last)

        def pre_src(idx):
            return (xb, R_XB) if idx < 8 else (tb, R_TB)

        main_x_loaded = {"done": False}

        def load_main_x(p):
            src = xT[:, p * T:(p + 1) * T].rearrange("(k p) t -> p k t", p=128)
            P.op("pool", lambda e, src=src: e.dma_start(out=xb[:], in_=src), writes=R_XB, dsem=s_xb)

        def late_setup_loads():
            P.op("pool", lambda e: e.dma_start(out=wa[:], in_=w_a.rearrange("h i j -> i h j")), writes=[R_WS], dsem=s_ws)
            P.op("pool", lambda e: e.dma_start(out=wx[:], in_=w_x.rearrange("h i j -> i h j")), writes=[R_WS], dsem=s_ws)
            P.op("pool", lambda e: e.dma_start(out=wp[:], in_=w_pool.rearrange("g (k p) n -> p g k n", p=128)),
                 writes=[R_WS], dsem=s_ws)
            for hh in range(2):
                P.op("pool", lambda e, hh=hh: e.dma_start(
                    out=GB[hh][:, :].rearrange("p (k n) -> p k n", n=256),
                    in_=memT[hh * 1024:(hh + 1) * 1024, :].rearrange("(k p) n -> p k n", p=128)),
                    writes=[R_MEMT], dsem=s_mem)

        def pre_hook(step):
            if step == -1:
                late_setup_loads()
            if step == -2:
                P.op("pool", lambda e: e.dma_start(out=xhb[:], in_=xhT.rearrange("(k p) t -> p k t", p=128)),
                     writes=[R_XH], dsem=s_xh)
            if 0 <= step < 6:
                for k in range(3 * step, min(3 * step + 3, KC)):
                    srck = xpT[k * 128:(k + 1) * 128, T:2 * T]
                    P.op("pool", lambda e, k=k, srck=srck: e.dma_start(out=tb[:, k, :], in_=srck),
                         writes=[R_TB[k]], dsem=s_tbx)
                if step == 5:
                    for k in range(KC):
                        R_TB[k].w = (s_tbx, s_tbx.n)
            if 6 <= step < 14:
                for k in (2 * (step - 6), 2 * (step - 6) + 1):
                    srck = xT[k * 128:(k + 1) * 128, 0:T]
                    P.op("pool", lambda e, k=k, srck=srck: e.dma_start(out=xb[:, k, :], in_=srck),
                         writes=[R_XB[k]], dsem=s_xb)
                if step == 13:
                    for k in range(KC):
                        R_XB[k].w = (s_xb, s_xb.n)

        src0 = xpT[:, 0:T].rearrange("(k p) t -> p k t", p=128)
        P.op("pool", lambda e: e.dma_start(out=xb[:], in_=src0), writes=R_XB, dsem=s_xb)
        lru_loop(16, pre_src, False, lambda i: "always" if i < 8 else None, kv=True, hook=pre_hook)
        if kv_state["pending"] is not None:
            kv_state["pending"]()
            kv_state["pending"] = None
        dve(lambda e: e.tensor_scalar(out=car_h[:], in0=car_h[:], scalar1=col(pp, C_FLAG), scalar2=None, op0=ALU.mult),
            R_CH + [R_PP], R_CH)

        gate_done = {}
        gate_slot = {}

        def make_gate_tasks():
            def mk(j):
                def task():
                    if j % 2 == 0:
                        gate_slot["s"] = load_block(w_in, 0, 2048 + (j // 2) * BW)
                    di = nextD()
                    proj(di, gate_slot["s"], (j % 2) * 128, xb, R_XB)
                    act(tb[:, 8 + j, :], PD[di][:, :], AF.Gelu_apprx_tanh, [R_PD[di]], [R_TB[8 + j]])
                return task
            return [mk(j) for j in range(8)]

        for p in range(NPASS):
            t0 = p * T
            P.inherit(R_XF, ALL_SCR)
            src = xT[:, t0:t0 + T].rearrange("(k p) t -> p k t", p=128)
            if not gate_done.get(p, False):
                for task in make_gate_tasks():
                    task()
            pool_slot = {}

            def mbv(cc):
                return tb[:, cc, :], R_TB[cc]

            def pool_halo(g, h):
                if h == 0:
                    pool_slot["s"] = load_block(w_in, 0, g * BW)
                slot = pool_slot["s"]
                cc = 2 * g + h
                if p == 0:
                    mm_big(1, lambda k, slot=slot, h=h: ring[:, slot, k, h * 128:(h + 1) * 128],
                           lambda k, tt: xhb[:, k, :], KC, reads=[R_RING[slot], R_XH], ntile=1, n=16)
                    act(car_p[:, cc, :], PD[1][:, 0:16], AF.Copy, [R_PD[1]], [R_CP[cc]])

            def pool_proj(g, h):
                proj(1, pool_slot["s"], h * 128, xb, R_XB)

            def pool_rest(g, h):
                w = 2 ** (g + 1)
                cc = 2 * g + h
                RUP = R_SV["up"][h]
                RQ, RR = R_SV["Q"][0], R_SV["R"][0]
                dve(lambda e, cc=cc, h=h: e.tensor_copy(out=sv("up", h, 0, 16), in_=car_p[:, cc, :]), [R_CP[cc]], [RUP])
                act(sv("up", h, 16, T), PD[1][:, :], AF.Copy, [R_PD[1]], [RUP])
                dve(lambda e, cc=cc, h=h: e.tensor_copy(out=car_p[:, cc, :], in_=sv("up", h, T, 16)), [RUP], [R_CP[cc]])
                srcn, srci, srcr = "up", h, RUP
                sh = 1
                for st in range(g + 1):
                    dstn, dstr = ("Q", RQ) if st % 2 == 0 else ("R", RR)
                    lo = 2 * sh - 1
                    dve(lambda e, srcn=srcn, srci=srci, dstn=dstn, lo=lo, sh=sh: e.tensor_tensor(
                        out=sv(dstn, 0, lo, 1040 - lo), in0=sv(srcn, srci, lo, 1040 - lo),
                        in1=sv(srcn, srci, lo - sh, 1040 - lo), op=ALU.add), [srcr], [dstr])
                    srcn, srci, srcr = dstn, 0, dstr
                    sh *= 2
                mv, mr = mbv(cc)
                dve(lambda e, srcn=srcn, srci=srci, mv=mv, w=w, h=h: e.scalar_tensor_tensor(
                    out=mv, in0=sv(srcn, srci, 16, T), scalar=1.0 / w, in1=sv("up", h, 16, T),
                    op0=ALU.mult, op1=ALU.subtract), [srcr, RUP], [mr])
                if p == 0:
                    dve(lambda e, srcn=srcn, srci=srci, g=g: e.tensor_tensor(
                        out=tmp16[:], in0=sv(srcn, srci, 16, 16), in1=pp[:, C_RC + 16 * g:C_RC + 16 * g + 16],
                        op=ALU.mult), [srcr, R_PP], [R_T16])
                    dve(lambda e, mv=mv, h=h: e.tensor_tensor(out=mv[:, 0:16], in0=tmp16[:], in1=sv("up", h, 16, 16),
                                                              op=ALU.subtract), [R_T16, RUP], [mr])

            def pool_B(g):
                dis = []
                for h in range(2):
                    di = nextD()
                    dis.append(di)
                    mm_big(di, lambda k, g=g, h=h: wp[:, g, k, h * 128:(h + 1) * 128],
                           lambda k, tt, g=g: tb[:, 2 * g + k, tt * 512:(tt + 1) * 512], 2,
                           reads=[R_WS, R_TB[2 * g], R_TB[2 * g + 1]])
                for h in range(2):
                    oc = 2 * g + h
                    act(tb[:, oc, :], PD[dis[h]][:, :], AF.Identity, [R_PD[dis[h]], R_PP, R_DP], [R_TB[oc]],
                        bias=col(dp, P_BPS + oc), scale=col(pp, C_PSCALE + oc))

            def side0(step):
                if kv_tasks:
                    pe, ev = kv_tasks.pop(0)
                    pe()
                    ev(on_dve=False)
                cc = step + 2
                if 0 <= cc < 8:
                    pool_halo(cc // 2, cc % 2)

            def side1(step):
                cc = step + 2
                if 0 <= cc < 8:
                    pool_proj(cc // 2, cc % 2)

            def side2(step):
                cc = step + 2
                if 0 <= cc < 8:
                    pool_rest(cc // 2, cc % 2)

            lru_loop(8, lambda i: (xb, R_XB), True, (lambda i: "flag") if p == 0 else (lambda i: None),
                     side=(side0, side1, side2))
            for g in range(4):
                pool_B(g)
            if p == 0:
                assert not kv_tasks
                P.inherit([R_MEMT], R_G[0:2])
            P.inherit(ALL_SCR, R_XF)
            for c in range(KC):
                srcc = xT[c * 128:(c + 1) * 128, t0:t0 + T]
                P.op("sp", lambda e, c=c, srcc=srcc: e.dma_start(out=xfc(c), in_=srcc), writes=[R_XF[c]], dsem=s_xfc[c])
            dense_to_resid(w_out, 0, ln_last=False)
            def q_evac(c, di):
                act(tb[:, c, :], PD[di][:, :], AF.Copy, [R_PD[di]], [R_TB[c]])

            layer_norm(0, False, t0, pipe=(w_q, 0, q_evac))

            slot = None
            for c in range(4, KC):
                if c % 2 == 0:
                    slot = load_block(w_q, 0, (c // 2) * BW)
                di = nextD()
                proj(di, slot, (c % 2) * 128, xb, R_XB)
                q_evac(c, di)
            sc = 512.0 ** -0.5

            def ebuf(hd, mh):
                if hd % 2 == 0:
                    return Bq[mh][:, :], R_B[mh]
                return GB[2][:, mh * T:(mh + 1) * T], R_G[2]

            def att_S(hd):
                for mh in range(2):
                    eb, er = ebuf(hd, mh)
                    di = nextD()
                    mm_big(di, lambda k, hd=hd, mh=mh: kT[:, 4 * hd + k, mh * 128:(mh + 1) * 128],
                           lambda k, tt, hd=hd: tb[:, 4 * hd + k, tt * 512:(tt + 1) * 512], 4,
                           reads=[R_KT] + R_TB[4 * hd:4 * hd + 4])
                    act(eb, PD[di][:, :], AF.Exp, [R_PD[di]], [er], scale=sc)

            def att_V(hd):
                e0, r0 = ebuf(hd, 0)
                e1, r1 = ebuf(hd, 1)
                ebs = (e0, e1)
                di = nextD()
                mm_big(di, lambda k: ones[:], lambda k, tt: ebs[k][:, tt * 512:(tt + 1) * 512], 2,
                       reads=[R_ONES, r0, r1])
                gi = hd % 2
                act(G[gi][:], PD[di][:, :], AF.Ln, [R_PD[di]], [R_G[gi]])
                act(G[gi][:], G[gi][:], AF.Exp, [R_G[gi]], [R_G[gi]], scale=-1.0)
                for dc in range(4):
                    c = 4 * hd + dc
                    di = nextD()
                    mm_big(di, lambda k, c=c: vv[:, k, c * 128:(c + 1) * 128],
                           lambda k, tt: ebs[k][:, tt * 512:(tt + 1) * 512], 2, reads=[R_V, r0, r1])
                    dve(lambda e, c=c, di=di, gi=gi: e.tensor_tensor(out=tb[:, c, :], in0=PD[di][:, :], in1=G[gi][:],
                                                                     op=ALU.mult), [R_PD[di], R_G[gi]], [R_TB[c]])

            att_S(0)
            for hd in range(4):
                if hd + 1 < 4:
                    att_S(hd + 1)
                att_V(hd)
            dense_to_resid(w_o, 0, ln_last=False)
            def ff1_evac(j, di):
                gi = j % 2
                act(G[gi][:], PD[di][:, :], AF.Relu, [R_PD[di]], [R_G[gi]])
                dve(lambda e, j=j, gi=gi: e.tensor_tensor(out=tb[:, j, :], in0=G[gi][:], in1=G[gi][:], op=ALU.mult),
                    [R_G[gi]], [R_TB[j]])

            layer_norm(1, False, t0, pipe=(w_ff1, 0, ff1_evac))

            for r in range(4):
                slot = None
                for j in range(4 if r == 0 else 0, KC):
                    hc = 16 * r + j
                    if j % 2 == 0:
                        slot = load_block(w_ff1, 0, (hc // 2) * BW)
                    di = nextD()
                    proj(di, slot, (hc % 2) * 128, xb, R_XB)
                    ff1_evac(j, di)
                if r == 3 and p + 1 < NPASS:
                    load_main_x(p + 1)
                dense_to_resid(w_ff2, r * D, first=(r == 0), ln_last=(True if r == 3 else None))
            if p + 1 < NPASS:
                side = make_gate_tasks()
                layer_norm(2, True, t0, side=side)
                while side:
                    side.pop(0)()
                gate_done[p + 1] = True
            else:
                layer_norm(2, True, t0)

        def semof(k):
            return k.h if isinstance(k, DSem) else sem_cnt[k]

        def emit(name, e):
            for waits, fn, ev in P.q[name]:
                for k, v in waits:
                    e.wait_ge(semof(k), v)
                ins = fn(e)
                k, v = ev
                ins.then_inc(semof(k), 16 if isinstance(k, DSem) else 1)

        with nc.Block() as block:
            @block.tensor
            def _(e):
                emit("pe", e)

            @block.scalar
            def _(e):
                emit("act", e)

            @block.vector
            def _(e):
                emit("dve", e)

            @block.gpsimd
            def _(e):
                emit("pool", e)

            @block.sync
            def _(e):
                emit("sp", e)
                e.wait_ge(s_out.h, s_out.n)
    return nc


def _prep_inputs(inputs):
    f = lambda a: np.ascontiguousarray(np.asarray(a, dtype=np.float32))
    x = f(inputs["x"])
    mem = f(inputs["mem"])
    shared = {
        "w_in": f(inputs["w_in"][0]), "w_out": f(inputs["w_out"][0]), "w_q": f(inputs["w_q"][0]),
        "w_k": f(inputs["w_k"][0]), "w_v": f(inputs["w_v"][0]), "w_o": f(inputs["w_o"][0]),
        "w_ff1": f(inputs["w_ff1"][0]), "w_ff2": f(inputs["w_ff2"][0]),
        "w_a": f(inputs["w_a"][0]), "w_x": f(inputs["w_x"][0]), "w_pool": f(inputs["w_pool"][0]),
    }
    pp = np.zeros((128, NPP), np.float32)

    def cols(v, n):
        return np.asarray(v, np.float32).reshape(n, 128).T

    cw = np.asarray(inputs["conv_w"][0], np.float32)
    pp[:, C_CONVW:C_CONVW + 32] = cw.reshape(4, 8, 128).transpose(2, 1, 0).reshape(128, 32)
    pp[:, C_CONVB:C_CONVB + 8] = cols(inputs["conv_b"][0], 8)
    pp[:, C_BA:C_BA + 8] = cols(np.asarray(inputs["b_a"][0]).reshape(-1), 8)
    pp[:, C_BX:C_BX + 8] = cols(np.asarray(inputs["b_x"][0]).reshape(-1), 8)
    pp[:, C_LAM:C_LAM + 8] = cols(inputs["lru_lambda"][0], 8)
    pp[:, C_BPOOL:C_BPOOL + 8] = cols(np.asarray(inputs["b_pool"][0]).reshape(-1), 8)
    pp[:, C_PSCALE:C_PSCALE + 8] = cols(inputs["pool_scale"][0], 8)
    for i, nm in enumerate(("ln1_g", "ln1_b", "ln2_g", "ln2_b", "ln3_g", "ln3_b")):
        pp[:, C_LN + 16 * i:C_LN + 16 * (i + 1)] = cols(inputs[nm][0], 16)
    in_maps = []
    for core in range(NCORE):
        b, half = core // 2, core % 2
        m = dict(shared)
        m["xT"] = np.ascontiguousarray(x[b, half * TOK:(half + 1) * TOK, :].T)
        m["xpT"] = np.ascontiguousarray(x[b, 0:TOK, :].T) if half == 1 else np.zeros((D, TOK), np.float32)
        m["memT"] = np.ascontiguousarray(mem[b].T)
        m["xhT"] = np.ascontiguousarray(x[b, TOK - 16:TOK, :].T) if half == 1 else np.zeros((D, 16), np.float32)
        ppc = pp.copy()
        ppc[:, C_FLAG] = float(half)
        ppc[:, C_OMF] = 1.0 - float(half)
        for g in range(4):
            w = 2 ** (g + 1)
            for t in range(16):
                ppc[:, C_RC + 16 * g + t] = (1.0 / min(t + 1, w)) if half == 0 else (1.0 / w)
        m["pp"] = ppc
        in_maps.append(m)
    return in_maps


_NC_CACHE = {}


def kernel(**inputs):
    in_maps = _prep_inputs(inputs)
    if "nc" not in _NC_CACHE:
        _NC_CACHE["nc"] = build_nc()
    nc = _NC_CACHE["nc"]
    res = run_bass_kernel_spmd(nc, in_maps, core_ids=list(range(NCORE)))
    out = np.empty((NB, SEQ, D), np.float32)
    for core in range(NCORE):
        b, half = core // 2, core % 2
        out[b, half * TOK:(half + 1) * TOK, :] = res.results[core]["outT"].T
    return out
```

```python
import contextlib
import numpy as np
import concourse.bass as bass
import concourse.mybir as mybir
from concourse.bass_utils import run_bass_kernel_spmd

F32 = mybir.dt.float32
BF16 = mybir.dt.bfloat16
ALU = mybir.AluOpType
AF = mybir.ActivationFunctionType

D = 2048
SEQ = 4096
NB = 4
NCORE = 8
TOK = 2048
T = 1024
NPASS = 2
KC = 16
ALPHA = 2.0 ** 0.25
LN_EPS = 1e-5
NPP = 256
BW = 256
NSLOT = 4

C_CONVW, C_CONVB, C_BA, C_BX, C_LAM, C_BPOOL, C_PSCALE = 0, 32, 40, 48, 56, 64, 72
C_LN = 80
C_FLAG, C_OMF, C_RC = 176, 177, 178
P_CH, P_C, P_HBA, P_HBX, P_BPS, P_E, P_L = 0, 8, 16, 24, 32, 40, 48


class Res:
    __slots__ = ("name", "w", "r")

    def __init__(self, name):
        self.name = name
        self.w = None
        self.r = {}


class DSem:
    def __init__(self, handle):
        self.h = handle
        self.n = 0


class Prog:
    ENG = ("pe", "act", "dve", "pool", "sp")

    def __init__(self):
        self.q = {e: [] for e in self.ENG}
        self.cnt = {e: 0 for e in self.ENG}
        self.waited = {e: {} for e in self.ENG}

    def op(self, eng, fn, reads=(), writes=(), dsem=None):
        raw = {}
        oth = {}

        def add(d, ev):
            if ev is None:
                return
            k, v = ev
            if d.get(k, 0) < v:
                d[k] = v

        for r in reads:
            add(raw, r.w)
        for w in writes:
            add(oth, w.w)
            for k, v in w.r.items():
                add(oth, (k, v))
        if dsem is not None:
            dsem.n += 16
            ev = (dsem, dsem.n)
        else:
            self.cnt[eng] += 1
            ev = (eng, self.cnt[eng])
        deps = dict(raw)
        for k, v in oth.items():
            if k == eng and dsem is None:
                continue
            if deps.get(k, 0) < v:
                deps[k] = v
        waits = []
        wd = self.waited[eng]
        for k, v in deps.items():
            if wd.get(k, 0) >= v:
                continue
            wd[k] = v
            waits.append((k, v))
        self.q[eng].append((waits, fn, ev))
        for r in reads:
            if r.r.get(ev[0], 0) < ev[1]:
                r.r[ev[0]] = ev[1]
        for w in writes:
            w.w = ev
            w.r = {}
        return ev

    @staticmethod
    def inherit(srcs, dsts):
        ev = {}
        for s in srcs:
            if s.w is not None and ev.get(s.w[0], 0) < s.w[1]:
                ev[s.w[0]] = s.w[1]
            for k, v in s.r.items():
                if ev.get(k, 0) < v:
                    ev[k] = v
        for d in dsts:
            for k, v in ev.items():
                if d.r.get(k, 0) < v:
                    d.r[k] = v


def build_nc(stop=None):
    nc = bass.Bass("TRN2", target_bir_lowering=False)
    P = Prog()

    def dram(name, shape, kind="ExternalInput"):
        return nc.dram_tensor(name, list(shape), F32, kind=kind).ap()

    xT = dram("xT", [D, TOK])
    xpT = dram("xpT", [D, TOK])
    memT = dram("memT", [D, 256])
    pp_d = dram("pp", [128, NPP])
    w_in = dram("w_in", [D, 3072])
    w_out = dram("w_out", [D, D])
    w_q = dram("w_q", [D, D])
    w_k = dram("w_k", [D, D])
    w_v = dram("w_v", [D, D])
    w_o = dram("w_o", [D, D])
    w_ff1 = dram("w_ff1", [D, 4 * D])
    w_ff2 = dram("w_ff2", [4 * D, D])
    w_a = dram("w_a", [8, 128, 128])
    w_x = dram("w_x", [8, 128, 128])
    w_pool = dram("w_pool", [4, 256, 256])
    xhT = dram("xhT", [D, 16])
    outT = dram("outT", [D, TOK], kind="ExternalOutput")

    es = contextlib.ExitStack()
    with es:
        def sb(name, shape, dt):
            return es.enter_context(nc.sbuf_tensor(name, list(shape), dt))

        xf = sb("xf", [128, KC * T + 128], F32)
        xb = sb("xb", [128, KC, T], BF16)
        tb = sb("tb", [128, KC, T], BF16)
        ring = sb("ring", [128, NSLOT, KC, BW], BF16)
        kT = sb("kT", [128, KC, 256], BF16)
        vv = sb("vv", [128, 2, D], BF16)
        wa = sb("wa", [128, 8, 128], BF16)
        wx = sb("wx", [128, 8, 128], BF16)
        wp = sb("wp", [128, 4, 2, 256], BF16)
        pp = sb("ppar", [128, NPP], F32)
        dp = sb("dpar", [128, 64], F32)
        ones = sb("ones", [128, 128], BF16)
        car_h = sb("car_h", [128, 8], F32)
        car_u = sb("car_u", [128, 8, 4], F32)
        car_p = sb("car_p", [128, 8, 16], F32)
        tmp16 = sb("tmp16", [128, 16], F32)
        ksum = sb("ksum", [128, 16], F32)
        G = [sb(f"G{i}", [128, T], F32) for i in range(4)]
        Bq = [sb(f"B{i}", [128, T], BF16) for i in range(2)]
        xhb = sb("xhb", [128, KC, 16], BF16)
        PD = [es.enter_context(nc.psum_tensor(f"pd{i}", [128, T], F32)) for i in range(4)]

        sem_cnt = {e: es.enter_context(nc.semaphore(f"c_{e}")) for e in ("pe", "act", "dve", "pool")}

        def dsem(name):
            return DSem(es.enter_context(nc.semaphore(name)))

        s_ring = [dsem(f"ring{i}") for i in range(NSLOT)]
        s_xb = dsem("xbld")
        s_xfc = [dsem(f"xfld{c}") for c in range(KC)]
        s_out = dsem("outst")
        s_pp = dsem("ppld")
        s_ws = dsem("wsld")
        s_mem = dsem("memld")
        s_xh = dsem("xhld")
        s_tbx = dsem("tbxld")

        R_XF = [Res(f"xf{c}") for c in range(KC)]
        R_XB = [Res(f"xb{c}") for c in range(KC)]
        R_TB = [Res(f"tb{c}") for c in range(KC)]
        R_RING = [Res(f"ring{i}") for i in range(NSLOT)]
        R_KT, R_V, R_WS, R_PP, R_DP, R_ONES = Res("kT"), Res("v"), Res("ws"), Res("pp"), Res("dp"), Res("ones")
        R_MEMT = Res("memT")
        R_CH = [Res(f"ch{j}") for j in range(8)]
        R_CU = [Res(f"cu{j}") for j in range(8)]
        R_CP = [Res(f"cp{j}") for j in range(8)]
        R_T16 = Res("t16")
        R_KS = Res("ksum")
        R_G = [Res(f"G{i}") for i in range(4)]
        R_B = [Res(f"B{i}") for i in range(2)]
        R_XH = Res("xh")
        R_PD = [Res(f"pd{i}") for i in range(4)]
        VOFF = {"u": (0, 1040, 3), "xc": (3120, 1024, 3), "tha": (6192, 1024, 2), "a": (8240, 1024, 2),
                "thi": (10288, 1024, 2), "up": (12336, 1040, 2), "Q": (14416, 1040, 1), "R": (15456, 1040, 1)}
        R_SV = {n: [Res(f"{n}{i}") for i in range(VOFF[n][2])] for n in VOFF}
        ALL_SCR = [r for n in R_SV for r in R_SV[n]]

        def sv(name, i, lo, n):
            b = VOFF[name][0] + i * VOFF[name][1]
            return xf[:, b + lo:b + lo + n]

        def xfc(c):
            return xf[:, c * T:(c + 1) * T]

        GB = [g.bitcast(BF16) for g in G]

        def memv(k, lo, n):
            return GB[k // 8][:, (k % 8) * 256 + lo:(k % 8) * 256 + lo + n]

        state = {"d": 0, "slot": 0, "dset": [0, 1, 2, 3]}

        def nextD():
            ds = state["dset"]
            state["d"] = (state["d"] + 1) % len(ds)
            return ds[state["d"]]

        def load_block(w_ap, r0, c0):
            s = state["slot"]
            state["slot"] = (s + 1) % NSLOT
            src = w_ap[r0:r0 + D, c0:c0 + BW].rearrange("(k p) n -> p k n", p=128)
            P.op("pool", lambda e, s=s, src=src: e.dma_start(out=ring[:, s], in_=src),
                 writes=[R_RING[s]], dsem=s_ring[s])
            return s

        def mm_big(di, lhs_fn, rhs_fn, nk, reads, ntile=2, n=512):
            def fn(e):
                ins = None
                for k in range(nk):
                    l = lhs_fn(k)
                    for tt in range(ntile):
                        ins = e.matmul(PD[di][:, tt * 512:tt * 512 + n], lhsT=l, rhs=rhs_fn(k, tt),
                                       start=(k == 0), stop=(k == nk - 1))
                return ins
            P.op("pe", fn, reads=reads, writes=[R_PD[di]])

        def proj(di, slot, off, src, R_SRC):
            mm_big(di, lambda k: ring[:, slot, k, off:off + 128],
                   lambda k, tt: src[:, k, tt * 512:(tt + 1) * 512], KC,
                   reads=[R_RING[slot]] + R_SRC)

        def act(out, in_, func, reads, writes, bias=None, scale=None):
            kw = {}
            if bias is not None:
                kw["bias"] = bias
            if scale is not None:
                kw["scale"] = scale
            P.op("act", lambda e: e.activation(out=out, in_=in_, func=func, **kw), reads=reads, writes=writes)

        def dve(fn, reads, writes):
            P.op("dve", fn, reads=reads, writes=writes)

        def col(t, c):
            return t[:, c:c + 1]

        P.op("sp", lambda e: e.dma_start(out=pp[:], in_=pp_d[:, :]), writes=[R_PP], dsem=s_pp)
        dve(lambda e: e.memset(ones[:], 1.0), [], [R_ONES])
        dve(lambda e: e.memset(car_h[:], 0.0), [], R_CH)
        dve(lambda e: e.memset(car_u[:], 0.0), [], R_CU)
        act(dp[:, P_E:P_E + 8], pp[:, C_LAM:C_LAM + 8], AF.Exp, [R_PP], [R_DP], scale=-1.0)
        act(dp[:, P_L:P_L + 8], dp[:, P_E:P_E + 8], AF.Ln, [R_DP], [R_DP], bias=1.0)
        dve(lambda e: e.tensor_scalar(out=dp[:, P_CH:P_CH + 8], in0=dp[:, P_L:P_L + 8], scalar1=-4.0, scalar2=None,
                                      op0=ALU.mult), [R_DP], [R_DP])
        dve(lambda e: e.tensor_scalar(out=dp[:, P_C:P_C + 8], in0=dp[:, P_L:P_L + 8], scalar1=-8.0, scalar2=None,
                                      op0=ALU.mult), [R_DP], [R_DP])
        dve(lambda e: e.tensor_scalar(out=dp[:, P_HBA:P_HBA + 16], in0=pp[:, C_BA:C_BA + 16], scalar1=0.5, scalar2=None,
                                      op0=ALU.mult), [R_PP], [R_DP])
        dve(lambda e: e.tensor_tensor(out=dp[:, P_BPS:P_BPS + 8], in0=pp[:, C_BPOOL:C_BPOOL + 8],
                                      in1=pp[:, C_PSCALE:C_PSCALE + 8], op=ALU.mult), [R_PP], [R_DP])

        kv_tasks = []

        def kv_k(cb):
            def pe():
                s = load_block(w_k, 0, cb * BW)

                def fnk(e, s=s):
                    ins = None
                    for h in range(2):
                        for k in range(KC):
                            ins = e.matmul(PD[1][:, h * 256:(h + 1) * 256], lhsT=ring[:, s, k, h * 128:(h + 1) * 128],
                                           rhs=memv(k, 0, 256), start=(k == 0), stop=(k == KC - 1))
                    return ins
                P.op("pe", fnk, reads=[R_RING[s], R_MEMT], writes=[R_PD[1]])

            def ev(on_dve=False):
                for h in range(2):
                    c = cb * 2 + h
                    src = PD[1][:, h * 256:(h + 1) * 256]
                    dve(lambda e, c=c, src=src: e.reduce_sum(out=ksum[:, c:c + 1], in_=src, axis=mybir.AxisListType.X),
                        [R_PD[1]], [R_KS])
                    dve(lambda e, c=c: e.tensor_scalar(out=ksum[:, c:c + 1], in0=ksum[:, c:c + 1], scalar1=1.0 / 256,
                                                       scalar2=None, op0=ALU.mult), [R_KS], [R_KS])
                    dve(lambda e, c=c, src=src: e.tensor_scalar(out=kT[:, c, :], in0=src, scalar1=ksum[:, c:c + 1],
                                                                scalar2=None, op0=ALU.subtract), [R_PD[1], R_KS], [R_KT])
            return pe, ev

        def kv_v(cb):
            def pe():
                s = load_block(w_v, 0, cb * BW)

                def fnv(e, s=s):
                    ins = None
                    for mh in range(2):
                        for k in range(KC):
                            ins = e.matmul(PD[1][:, mh * 256:(mh + 1) * 256], lhsT=memv(k, mh * 128, 128),
                                           rhs=ring[:, s, k, :], start=(k == 0), stop=(k == KC - 1))
                    return ins
                P.op("pe", fnv, reads=[R_RING[s], R_MEMT], writes=[R_PD[1]])

            def ev(on_dve=False):
                for mh in range(2):
                    if on_dve:
                        dve(lambda e, mh=mh: e.tensor_copy(out=vv[:, mh, cb * BW:(cb + 1) * BW],
                                                           in_=PD[1][:, mh * 256:(mh + 1) * 256]), [R_PD[1]], [R_V])
                    else:
                        act(vv[:, mh, cb * BW:(cb + 1) * BW], PD[1][:, mh * 256:(mh + 1) * 256], AF.Copy,
                            [R_PD[1]], [R_V])
            return pe, ev

        for cb in range(D // BW):
            kv_tasks.append(kv_k(cb))
        for cb in range(D // BW):
            kv_tasks.append(kv_v(cb))

        lru_slot = {}

        def lru_Ppe(idx, srcsel):
            j = idx % 8
            if j % 2 == 0:
                lru_slot["s"] = load_block(w_in, 0, 1024 + (j // 2) * BW)
            src, rsrc = srcsel(idx)
            proj(0, lru_slot["s"], (j % 2) * 128, src, rsrc)

        def lru_Pact(idx):
            j, ui = idx % 8, idx % 3
            dve(lambda e: e.tensor_copy(out=sv("u", ui, 12, 4), in_=car_u[:, j, :]), [R_CU[j]], [R_SV["u"][ui]])
            act(sv("u", ui, 16, T), PD[0][:, :], AF.Copy, [R_PD[0]], [R_SV["u"][ui]])

        def lru_Ca(idx, tap0_dve=False):
            j = idx % 8
            ui, xi = idx % 3, idx % 3
            RU, RX = R_SV["u"][ui], R_SV["xc"][xi]
            cw = C_CONVW + 4 * j
            if tap0_dve:
                dve(lambda e: e.tensor_scalar(out=sv("xc", xi, 0, T), in0=sv("u", ui, 13, T), scalar1=col(pp, cw),
                                              scalar2=col(pp, C_CONVB + j), op0=ALU.mult, op1=ALU.add),
                    [RU, R_PP], [RX])
            else:
                act(sv("xc", xi, 0, T), sv("u", ui, 13, T), AF.Identity, [RU, R_PP], [RX],
                    bias=col(pp, C_CONVB + j), scale=col(pp, cw))
            for k in range(1, 4):
                dve(lambda e, k=k: e.scalar_tensor_tensor(out=sv("xc", xi, 0, T), in0=sv("u", ui, 13 + k, T),
                                                          scalar=col(pp, cw + k), in1=sv("xc", xi, 0, T),
                                                          op0=ALU.mult, op1=ALU.add),
                    [RU, RX, R_PP], [RX])
            dve(lambda e: e.tensor_copy(out=car_u[:, j, :], in_=sv("u", ui, 12 + T, 4)), [RU], [R_CU[j]])

        def lru_Cb(idx):
            j = idx % 8
            xi, bi = idx % 3, idx % 2
            RX = R_SV["xc"][xi]
            act(Bq[bi][:], sv("xc", xi, 0, T), AF.Copy, [RX], [R_B[bi]])
            for (dd, wt) in ((2, wa), (3, wx)):
                mm_big(dd, lambda k, wt=wt: wt[:, j, :], lambda k, tt: Bq[bi][:, tt * 512:(tt + 1) * 512], 1,
                       reads=[R_WS, R_B[bi]])

        def lru_A1(idx):
            j, ti = idx % 8, idx % 2
            RT, RI = R_SV["tha"][ti], R_SV["thi"][ti]
            act(sv("tha", ti, 0, T), PD[2][:, :], AF.Tanh, [R_PD[2], R_DP], [RT],
                bias=col(dp, P_HBA + j), scale=0.5)
            act(sv("thi", ti, 0, T), PD[3][:, :], AF.Tanh, [R_PD[3], R_DP], [RI],
                bias=col(dp, P_HBX + j), scale=0.5)

        def lru_A2(idx, reset):
            j, ti = idx % 8, idx % 2
            RT, RI, RA = R_SV["tha"][ti], R_SV["thi"][ti], R_SV["a"][ti]
            act(sv("a", ti, 0, T), sv("tha", ti, 0, T), AF.Exp, [RT, R_DP], [RA],
                bias=col(dp, P_CH + j), scale=col(dp, P_CH + j))
            act(sv("tha", ti, 0, T), sv("tha", ti, 0, T), AF.Exp, [RT, R_DP], [RT],
                bias=col(dp, P_C + j), scale=col(dp, P_C + j))
            act(sv("tha", ti, 0, T), sv("tha", ti, 0, T), AF.Sqrt, [RT], [RT], bias=1.0, scale=-1.0)
            if reset == "always":
                dve(lambda e: e.memset(sv("tha", ti, 0, 1), 1.0), [], [RT])
            elif reset == "flag":
                dve(lambda e: e.tensor_scalar(out=sv("tha", ti, 0, 1), in0=sv("tha", ti, 0, 1), scalar1=col(pp, C_FLAG),
                                              scalar2=col(pp, C_OMF), op0=ALU.mult, op1=ALU.add),
                    [RT, R_PP], [RT])

        def lru_B(idx, main):
            j, ti, xi = idx % 8, idx % 2, idx % 3
            RT, RI, RA, RX = R_SV["tha"][ti], R_SV["thi"][ti], R_SV["a"][ti], R_SV["xc"][xi]
            dve(lambda e: e.scalar_tensor_tensor(out=sv("xc", xi, 0, T), in0=sv("thi", ti, 0, T), scalar=1.0,
                                                 in1=sv("xc", xi, 0, T), op0=ALU.add, op1=ALU.mult),
                [RI, RX], [RX])
            dve(lambda e: e.scalar_tensor_tensor(out=sv("xc", xi, 0, T), in0=sv("xc", xi, 0, T), scalar=0.5,
                                                 in1=sv("tha", ti, 0, T), op0=ALU.mult, op1=ALU.mult),
                [RX, RT], [RX])
            dve(lambda e: e.tensor_tensor_scan(out=sv("tha", ti, 0, T), data0=sv("a", ti, 0, T), data1=sv("xc", xi, 0, T),
                                               initial=car_h[:, j:j + 1], op0=ALU.mult, op1=ALU.add),
                [RA, RX, R_CH[j], RT], [RT])
            dve(lambda e: e.tensor_copy(out=car_h[:, j:j + 1], in_=sv("tha", ti, T - 1, 1)), [RT], [R_CH[j]])
            if main:
                dve(lambda e: e.tensor_tensor(out=tb[:, 8 + j, :], in0=sv("tha", ti, 0, T), in1=tb[:, 8 + j, :],
                                              op=ALU.mult), [RT, R_TB[8 + j]], [R_TB[8 + j]])

        def lru_loop(n, srcsel, main, reset_fn, kv=False, hook=None, side=None):
            for step in range(-2, n):
                if hook is not None:
                    hook(step)
                if side is not None:
                    side[0](step)
                if 0 <= step < n:
                    lru_A1(step)
                if 0 <= step + 2 < n:
                    lru_Ppe(step + 2, srcsel)
                if side is not None:
                    side[1](step)
                if 0 <= step + 1 < n:
                    lru_Ca(step + 1, tap0_dve=not main)
                if 0 <= step < n:
                    lru_A2(step, reset_fn(step))
                if 0 <= step + 1 < n:
                    lru_Cb(step + 1)
                if 0 <= step + 2 < n:
                    lru_Pact(step + 2)
                if 0 <= step < n:
                    lru_B(step, main)
                if side is not None:
                    side[2](step)
                if kv and step >= 0:
                    if kv_state["pending"] is not None:
                        kv_state["pending"]()
                        kv_state["pending"] = None
                    if kv_tasks:
                        pe, ev = kv_tasks.pop(0)
                        pe()
                        kv_state["pending"] = ev
            state["d"] = 0

        kv_state = {"pending": None}

        def ln_zb(c, last):
            if last:
                return GB[2 + c % 2][:, 0:T], R_G[2 + c % 2]
            return xb[:, c, :], R_XB[c]

        def ln_stats_prep(c, last):
            act(Bq[c % 2][:], xfc(c), AF.Square, [R_XF[c]], [R_B[c % 2]])
            zb, rzb = ln_zb(c, last)
            dve(lambda e, c=c, zb=zb: e.tensor_copy(out=zb, in_=xfc(c)), [R_XF[c]], [rzb])

        def ln_stats_pe(c, last):
            d1, d2 = 2, 3
            zb, rzb = ln_zb(c, last)

            def fn(e, c=c, zb=zb):
                ins = None
                for tt in range(2):
                    e.matmul(PD[d1][:, tt * 512:(tt + 1) * 512], lhsT=ones[:], rhs=zb[:, tt * 512:(tt + 1) * 512],
                             start=(c == 0), stop=(c == KC - 1))
                    ins = e.matmul(PD[d2][:, tt * 512:(tt + 1) * 512], lhsT=ones[:],
                                   rhs=Bq[c % 2][:, tt * 512:(tt + 1) * 512], start=(c == 0), stop=(c == KC - 1))
                return ins
            P.op("pe", fn, reads=[R_ONES, rzb, R_B[c % 2]], writes=[R_PD[d1], R_PD[d2]])

        def layer_norm(li, last, t0, pipe=None, side=None):
            d1, d2 = 2, 3
            inv = 1.0 / D
            dve(lambda e: e.tensor_scalar(out=G[0][:], in0=PD[d1][:, :], scalar1=inv, scalar2=None, op0=ALU.mult),
                [R_PD[d1]], [R_G[0]])
            dve(lambda e: e.tensor_tensor(out=G[2][:], in0=G[0][:], in1=G[0][:], op=ALU.mult), [R_G[0]], [R_G[2]])
            dve(lambda e: e.scalar_tensor_tensor(out=G[1][:], in0=PD[d2][:, :], scalar=inv, in1=G[2][:],
                                                 op0=ALU.mult, op1=ALU.subtract), [R_PD[d2], R_G[2]], [R_G[1]])
            act(G[1][:], G[1][:], AF.Ln, [R_G[1]], [R_G[1]], bias=LN_EPS)
            act(G[1][:], G[1][:], AF.Exp, [R_G[1]], [R_G[1]], scale=-0.5)
            state["dset"] = [0, 1, 2, 3]
            cg = C_LN + 32 * li
            NG = 4
            if pipe is not None:
                pw, pcol, pevac = pipe
                pslots = [load_block(pw, 0, pcol), load_block(pw, 0, pcol + BW)]
            for c in range(KC):
                ti = 2 + (c % 2)
                dve(lambda e, c=c, ti=ti: e.tensor_tensor(out=G[ti][:], in0=xfc(c), in1=G[0][:], op=ALU.subtract),
                    [R_XF[c], R_G[0]], [R_G[ti]])
                dve(lambda e, ti=ti: e.tensor_tensor(out=G[ti][:], in0=G[ti][:], in1=G[1][:], op=ALU.mult),
                    [R_G[ti], R_G[1]], [R_G[ti]])
                act(xfc(c), G[ti][:], AF.Identity, [R_G[ti], R_PP], [R_XF[c]],
                    bias=col(pp, cg + 16 + c), scale=col(pp, cg + c))
                if not last:
                    act(xb[:, c, :], G[ti][:], AF.Identity, [R_G[ti], R_PP], [R_XB[c]],
                        bias=col(pp, cg + 16 + c), scale=col(pp, cg + c))
                    if pipe is not None:
                        def fnp(e, c=c):
                            ins = None
                            for g in range(NG):
                                for tt in range(2):
                                    ins = e.matmul(PD[g][:, tt * 512:(tt + 1) * 512],
                                                   lhsT=ring[:, pslots[g // 2], c, (g % 2) * 128:(g % 2) * 128 + 128],
                                                   rhs=xb[:, c, tt * 512:(tt + 1) * 512],
                                                   start=(c == 0), stop=(c == KC - 1))
                            return ins
                        P.op("pe", fnp, reads=[R_XB[c], R_RING[pslots[0]], R_RING[pslots[1]]], writes=R_PD)
                else:
                    dst = outT[c * 128:(c + 1) * 128, t0:t0 + T]
                    P.op("sp", lambda e, c=c, dst=dst: e.dma_start(out=dst, in_=xfc(c)), reads=[R_XF[c]], dsem=s_out)
                if side and c % 2 == 1:
                    side.pop(0)()
            if last:
                for c in range(KC):
                    R_XF[c].r[s_out] = s_out.n
            if pipe is not None:
                for g in range(NG):
                    pevac(g, g)
                state["d"] = 3

        def residual(c, di, first=True):
            if first:
                dve(lambda e: e.scalar_tensor_tensor(out=xfc(c), in0=xfc(c), scalar=ALPHA, in1=PD[di][:, :],
                                                     op0=ALU.mult, op1=ALU.add), [R_XF[c], R_PD[di]], [R_XF[c]])
            else:
                dve(lambda e: e.tensor_tensor(out=xfc(c), in0=xfc(c), in1=PD[di][:, :], op=ALU.add),
                    [R_XF[c], R_PD[di]], [R_XF[c]])

        def dense_to_resid(w_ap, r0, first=True, ln_last=None):
            if ln_last is not None:
                state["dset"] = [0, 1]
                state["d"] = 0
            slot = None
            for c in range(KC):
                if c % 2 == 0:
                    slot = load_block(w_ap, r0, (c // 2) * BW)
                di = nextD()
                proj(di, slot, (c % 2) * 128, tb, R_TB)
                if ln_last is not None and c >= 1:
                    ln_stats_pe(c - 1, ln_last)
                residual(c, di, first)
                if ln_last is not None:
                    ln_stats_prep(c, ln_last)
            if ln_last is not None:
                ln_stats_pe(KC - 1, ln_last)

        def pre_src(idx):
            return (xb, R_XB) if idx < 8 else (tb, R_TB)

        main_x_loaded = {"done": False}

        def load_main_x(p):
            src = xT[:, p * T:(p + 1) * T].rearrange("(k p) t -> p k t", p=128)
            P.op("pool", lambda e, src=src: e.dma_start(out=xb[:], in_=src), writes=R_XB, dsem=s_xb)

        def late_setup_loads():
            P.op("pool", lambda e: e.dma_start(out=wa[:], in_=w_a.rearrange("h i j -> i h j")), writes=[R_WS], dsem=s_ws)
            P.op("pool", lambda e: e.dma_start(out=wx[:], in_=w_x.rearrange("h i j -> i h j")), writes=[R_WS], dsem=s_ws)
            P.op("pool", lambda e: e.dma_start(out=wp[:], in_=w_pool.rearrange("g (k p) n -> p g k n", p=128)),
                 writes=[R_WS], dsem=s_ws)
            for hh in range(2):
                P.op("pool", lambda e, hh=hh: e.dma_start(
                    out=GB[hh][:, :].rearrange("p (k n) -> p k n", n=256),
                    in_=memT[hh * 1024:(hh + 1) * 1024, :].rearrange("(k p) n -> p k n", p=128)),
                    writes=[R_MEMT], dsem=s_mem)

        def pre_hook(step):
            if step == -1:
                late_setup_loads()
            if step == -2:
                P.op("pool", lambda e: e.dma_start(out=xhb[:], in_=xhT.rearrange("(k p) t -> p k t", p=128)),
                     writes=[R_XH], dsem=s_xh)
            if 0 <= step < 6:
                for k in range(3 * step, min(3 * step + 3, KC)):
                    srck = xpT[k * 128:(k + 1) * 128, T:2 * T]
                    P.op("pool", lambda e, k=k, srck=srck: e.dma_start(out=tb[:, k, :], in_=srck),
                         writes=[R_TB[k]], dsem=s_tbx)
                if step == 5:
                    for k in range(KC):
                        R_TB[k].w = (s_tbx, s_tbx.n)
            if 6 <= step < 14:
                for k in (2 * (step - 6), 2 * (step - 6) + 1):
                    srck = xT[k * 128:(k + 1) * 128, 0:T]
                    P.op("pool", lambda e, k=k, srck=srck: e.dma_start(out=xb[:, k, :], in_=srck),
                         writes=[R_XB[k]], dsem=s_xb)
                if step == 13:
                    for k in range(KC):
                        R_XB[k].w = (s_xb, s_xb.n)

        src0 = xpT[:, 0:T].rearrange("(k p) t -> p k t", p=128)
        P.op("pool", lambda e: e.dma_start(out=xb[:], in_=src0), writes=R_XB, dsem=s_xb)
        lru_loop(16, pre_src, False, lambda i: "always" if i < 8 else None, kv=True, hook=pre_hook)
        if kv_state["pending"] is not None:
            kv_state["pending"]()
            kv_state["pending"] = None
        dve(lambda e: e.tensor_scalar(out=car_h[:], in0=car_h[:], scalar1=col(pp, C_FLAG), scalar2=None, op0=ALU.mult),
            R_CH + [R_PP], R_CH)

        gate_done = {}
        gate_slot = {}

        def make_gate_tasks():
            def mk(j):
                def task():
                    if j % 2 == 0:
                        gate_slot["s"] = load_block(w_in, 0, 2048 + (j // 2) * BW)
                    di = nextD()
                    proj(di, gate_slot["s"], (j % 2) * 128, xb, R_XB)
                    act(tb[:, 8 + j, :], PD[di][:, :], AF.Gelu_apprx_tanh, [R_PD[di]], [R_TB[8 + j]])
                return task
            return [mk(j) for j in range(8)]

        for p in range(NPASS):
            t0 = p * T
            P.inherit(R_XF, ALL_SCR)
            src = xT[:, t0:t0 + T].rearrange("(k p) t -> p k t", p=128)
            if not gate_done.get(p, False):
                for task in make_gate_tasks():
                    task()
            pool_slot = {}

            def mbv(cc):
                return tb[:, cc, :], R_TB[cc]

            def pool_halo(g, h):
                if h == 0:
                    pool_slot["s"] = load_block(w_in, 0, g * BW)
                slot = pool_slot["s"]
                cc = 2 * g + h
                if p == 0:
                    mm_big(1, lambda k, slot=slot, h=h: ring[:, slot, k, h * 128:(h + 1) * 128],
                           lambda k, tt: xhb[:, k, :], KC, reads=[R_RING[slot], R_XH], ntile=1, n=16)
                    act(car_p[:, cc, :], PD[1][:, 0:16], AF.Copy, [R_PD[1]], [R_CP[cc]])

            def pool_proj(g, h):
                proj(1, pool_slot["s"], h * 128, xb, R_XB)

            def pool_rest(g, h):
                w = 2 ** (g + 1)
                cc = 2 * g + h
                RUP = R_SV["up"][h]
                RQ, RR = R_SV["Q"][0], R_SV["R"][0]
                dve(lambda e, cc=cc, h=h: e.tensor_copy(out=sv("up", h, 0, 16), in_=car_p[:, cc, :]), [R_CP[cc]], [RUP])
                act(sv("up", h, 16, T), PD[1][:, :], AF.Copy, [R_PD[1]], [RUP])
                dve(lambda e, cc=cc, h=h: e.tensor_copy(out=car_p[:, cc, :], in_=sv("up", h, T, 16)), [RUP], [R_CP[cc]])
                srcn, srci, srcr = "up", h, RUP
                sh = 1
                for st in range(g + 1):
                    dstn, dstr = ("Q", RQ) if st % 2 == 0 else ("R", RR)
                    lo = 2 * sh - 1
                    dve(lambda e, srcn=srcn, srci=srci, dstn=dstn, lo=lo, sh=sh: e.tensor_tensor(
                        out=sv(dstn, 0, lo, 1040 - lo), in0=sv(srcn, srci, lo, 1040 - lo),
                        in1=sv(srcn, srci, lo - sh, 1040 - lo), op=ALU.add), [srcr], [dstr])
                    srcn, srci, srcr = dstn, 0, dstr
                    sh *= 2
                mv, mr = mbv(cc)
                dve(lambda e, srcn=srcn, srci=srci, mv=mv, w=w, h=h: e.scalar_tensor_tensor(
                    out=mv, in0=sv(srcn, srci, 16, T), scalar=1.0 / w, in1=sv("up", h, 16, T),
                    op0=ALU.mult, op1=ALU.subtract), [srcr, RUP], [mr])
                if p == 0:
                    dve(lambda e, srcn=srcn, srci=srci, g=g: e.tensor_tensor(
                        out=tmp16[:], in0=sv(srcn, srci, 16, 16), in1=pp[:, C_RC + 16 * g:C_RC + 16 * g + 16],
                        op=ALU.mult), [srcr, R_PP], [R_T16])
                    dve(lambda e, mv=mv, h=h: e.tensor_tensor(out=mv[:, 0:16], in0=tmp16[:], in1=sv("up", h, 16, 16),
                                                              op=ALU.subtract), [R_T16, RUP], [mr])

            def pool_B(g):
                dis = []
                for h in range(2):
                    di = nextD()
                    dis.append(di)
                    mm_big(di, lambda k, g=g, h=h: wp[:, g, k, h * 128:(h + 1) * 128],
                           lambda k, tt, g=g: tb[:, 2 * g + k, tt * 512:(tt + 1) * 512], 2,
                           reads=[R_WS, R_TB[2 * g], R_TB[2 * g + 1]])
                for h in range(2):
                    oc = 2 * g + h
                    act(tb[:, oc, :], PD[dis[h]][:, :], AF.Identity, [R_PD[dis[h]], R_PP, R_DP], [R_TB[oc]],
                        bias=col(dp, P_BPS + oc), scale=col(pp, C_PSCALE + oc))

            def side0(step):
                if kv_tasks:
                    pe, ev = kv_tasks.pop(0)
                    pe()
                    ev(on_dve=False)
                cc = step + 2
                if 0 <= cc < 8:
                    pool_halo(cc // 2, cc % 2)

            def side1(step):
                cc = step + 2
                if 0 <= cc < 8:
                    pool_proj(cc // 2, cc % 2)

            def side2(step):
                cc = step + 2
                if 0 <= cc < 8:
                    pool_rest(cc // 2, cc % 2)

            lru_loop(8, lambda i: (xb, R_XB), True, (lambda i: "flag") if p == 0 else (lambda i: None),
                     side=(side0, side1, side2))
            for g in range(4):
                pool_B(g)
            if p == 0:
                assert not kv_tasks
                P.inherit([R_MEMT], R_G[0:2])
            P.inherit(ALL_SCR, R_XF)
            for c in range(KC):
                srcc = xT[c * 128:(c + 1) * 128, t0:t0 + T]
                P.op("sp", lambda e, c=c, srcc=srcc: e.dma_start(out=xfc(c), in_=srcc), writes=[R_XF[c]], dsem=s_xfc[c])
            dense_to_resid(w_out, 0, ln_last=False)
            def q_evac(c, di):
                act(tb[:, c, :], PD[di][:, :], AF.Copy, [R_PD[di]], [R_TB[c]])

            layer_norm(0, False, t0, pipe=(w_q, 0, q_evac))

            slot = None
            for c in range(4, KC):
                if c % 2 == 0:
                    slot = load_block(w_q, 0, (c // 2) * BW)
                di = nextD()
                proj(di, slot, (c % 2) * 128, xb, R_XB)
                q_evac(c, di)
            sc = 512.0 ** -0.5

            def ebuf(hd, mh):
                if hd % 2 == 0:
                    return Bq[mh][:, :], R_B[mh]
                return GB[2][:, mh * T:(mh + 1) * T], R_G[2]

            def att_S(hd):
                for mh in range(2):
                    eb, er = ebuf(hd, mh)
                    di = nextD()
                    mm_big(di, lambda k, hd=hd, mh=mh: kT[:, 4 * hd + k, mh * 128:(mh + 1) * 128],
                           lambda k, tt, hd=hd: tb[:, 4 * hd + k, tt * 512:(tt + 1) * 512], 4,
                           reads=[R_KT] + R_TB[4 * hd:4 * hd + 4])
                    act(eb, PD[di][:, :], AF.Exp, [R_PD[di]], [er], scale=sc)

            def att_V(hd):
                e0, r0 = ebuf(hd, 0)
                e1, r1 = ebuf(hd, 1)
                ebs = (e0, e1)
                di = nextD()
                mm_big(di, lambda k: ones[:], lambda k, tt: ebs[k][:, tt * 512:(tt + 1) * 512], 2,
                       reads=[R_ONES, r0, r1])
                gi = hd % 2
                act(G[gi][:], PD[di][:, :], AF.Ln, [R_PD[di]], [R_G[gi]])
                act(G[gi][:], G[gi][:], AF.Exp, [R_G[gi]], [R_G[gi]], scale=-1.0)
                for dc in range(4):
                    c = 4 * hd + dc
                    di = nextD()
                    mm_big(di, lambda k, c=c: vv[:, k, c * 128:(c + 1) * 128],
                           lambda k, tt: ebs[k][:, tt * 512:(tt + 1) * 512], 2, reads=[R_V, r0, r1])
                    dve(lambda e, c=c, di=di, gi=gi: e.tensor_tensor(out=tb[:, c, :], in0=PD[di][:, :], in1=G[gi][:],
                                                                     op=ALU.mult), [R_PD[di], R_G[gi]], [R_TB[c]])

            att_S(0)
            for hd in range(4):
                if hd + 1 < 4:
                    att_S(hd + 1)
                att_V(hd)
            dense_to_resid(w_o, 0, ln_last=False)
            def ff1_evac(j, di):
                gi = j % 2
                act(G[gi][:], PD[di][:, :], AF.Relu, [R_PD[di]], [R_G[gi]])
                dve(lambda e, j=j, gi=gi: e.tensor_tensor(out=tb[:, j, :], in0=G[gi][:], in1=G[gi][:], op=ALU.mult),
                    [R_G[gi]], [R_TB[j]])

            layer_norm(1, False, t0, pipe=(w_ff1, 0, ff1_evac))

            for r in range(4):
                slot = None
                for j in range(4 if r == 0 else 0, KC):
                    hc = 16 * r + j
                    if j % 2 == 0:
                        slot = load_block(w_ff1, 0, (hc // 2) * BW)
                    di = nextD()
                    proj(di, slot, (hc % 2) * 128, xb, R_XB)
                    ff1_evac(j, di)
                if r == 3 and p + 1 < NPASS:
                    load_main_x(p + 1)
                dense_to_resid(w_ff2, r * D, first=(r == 0), ln_last=(True if r == 3 else None))
            if p + 1 < NPASS:
                side = make_gate_tasks()
                layer_norm(2, True, t0, side=side)
                while side:
                    side.pop(0)()
                gate_done[p + 1] = True
            else:
                layer_norm(2, True, t0)

        def semof(k):
            return k.h if isinstance(k, DSem) else sem_cnt[k]

        def emit(name, e):
            for waits, fn, ev in P.q[name]:
                for k, v in waits:
                    e.wait_ge(semof(k), v)
                ins = fn(e)
                k, v = ev
                ins.then_inc(semof(k), 16 if isinstance(k, DSem) else 1)

        with nc.Block() as block:
            @block.tensor
            def _(e):
                emit("pe", e)

            @block.scalar
            def _(e):
                emit("act", e)

            @block.vector
            def _(e):
                emit("dve", e)

            @block.gpsimd
            def _(e):
                emit("pool", e)

            @block.sync
            def _(e):
                emit("sp", e)
                e.wait_ge(s_out.h, s_out.n)
    return nc


def _prep_inputs(inputs):
    f = lambda a: np.ascontiguousarray(np.asarray(a, dtype=np.float32))
    x = f(inputs["x"])
    mem = f(inputs["mem"])
    shared = {
        "w_in": f(inputs["w_in"][0]), "w_out": f(inputs["w_out"][0]), "w_q": f(inputs["w_q"][0]),
        "w_k": f(inputs["w_k"][0]), "w_v": f(inputs["w_v"][0]), "w_o": f(inputs["w_o"][0]),
        "w_ff1": f(inputs["w_ff1"][0]), "w_ff2": f(inputs["w_ff2"][0]),
        "w_a": f(inputs["w_a"][0]), "w_x": f(inputs["w_x"][0]), "w_pool": f(inputs["w_pool"][0]),
    }
    pp = np.zeros((128, NPP), np.float32)

    def cols(v, n):
        return np.asarray(v, np.float32).reshape(n, 128).T

    cw = np.asarray(inputs["conv_w"][0], np.float32)
    pp[:, C_CONVW:C_CONVW + 32] = cw.reshape(4, 8, 128).transpose(2, 1, 0).reshape(128, 32)
    pp[:, C_CONVB:C_CONVB + 8] = cols(inputs["conv_b"][0], 8)
    pp[:, C_BA:C_BA + 8] = cols(np.asarray(inputs["b_a"][0]).reshape(-1), 8)
    pp[:, C_BX:C_BX + 8] = cols(np.asarray(inputs["b_x"][0]).reshape(-1), 8)
    pp[:, C_LAM:C_LAM + 8] = cols(inputs["lru_lambda"][0], 8)
    pp[:, C_BPOOL:C_BPOOL + 8] = cols(np.asarray(inputs["b_pool"][0]).reshape(-1), 8)
    pp[:, C_PSCALE:C_PSCALE + 8] = cols(inputs["pool_scale"][0], 8)
    for i, nm in enumerate(("ln1_g", "ln1_b", "ln2_g", "ln2_b", "ln3_g", "ln3_b")):
        pp[:, C_LN + 16 * i:C_LN + 16 * (i + 1)] = cols(inputs[nm][0], 16)
    in_maps = []
    for core in range(NCORE):
        b, half = core // 2, core % 2
        m = dict(shared)
        m["xT"] = np.ascontiguousarray(x[b, half * TOK:(half + 1) * TOK, :].T)
        m["xpT"] = np.ascontiguousarray(x[b, 0:TOK, :].T) if half == 1 else np.zeros((D, TOK), np.float32)
        m["memT"] = np.ascontiguousarray(mem[b].T)
        m["xhT"] = np.ascontiguousarray(x[b, TOK - 16:TOK, :].T) if half == 1 else np.zeros((D, 16), np.float32)
        ppc = pp.copy()
        ppc[:, C_FLAG] = float(half)
        ppc[:, C_OMF] = 1.0 - float(half)
        for g in range(4):
            w = 2 ** (g + 1)
            for t in range(16):
                ppc[:, C_RC + 16 * g + t] = (1.0 / min(t + 1, w)) if half == 0 else (1.0 / w)
        m["pp"] = ppc
        in_maps.append(m)
    return in_maps


_NC_CACHE = {}


def kernel(**inputs):
    in_maps = _prep_inputs(inputs)
    if "nc" not in _NC_CACHE:
        _NC_CACHE["nc"] = build_nc()
    nc = _NC_CACHE["nc"]
    res = run_bass_kernel_spmd(nc, in_maps, core_ids=list(range(NCORE)))
    out = np.empty((NB, SEQ, D), np.float32)
    for core in range(NCORE):
        b, half = core // 2, core % 2
        out[b, half * TOK:(half + 1) * TOK, :] = res.results[core]["outT"].T
    return out
```
